# Optimizing a Trainium2 kernel written in Bass

```python
import math
import jax, jax.numpy as jnp
from jax import lax
import numpy as np

D_MODEL = 1024
BATCH = 32
SEQ = 2048
DEPTH = 1

D_MIX = D_MODEL
D_HYENA = D_MIX // 2
HYENA_GROUPS = 8
D_ATTN = D_MIX - D_HYENA
N_HEADS = 8
HEAD_DIM = D_ATTN // N_HEADS
N_KV_HEADS = 2
Q_PER_KV = N_HEADS // N_KV_HEADS
KV_DIM = N_KV_HEADS * HEAD_DIM
WINDOW = 128
BLOCK = 128
SHORT_CONV = 3
FILTER_ORDER = 64
N_BANDS = 16
POS_EMB_DIM = 1 + 2 * N_BANDS
DECAY_TARGET = 1e-2
FAST_DECAY_PCT = 0.3
SLOW_DECAY_PCT = 1.5
RMS_EPS = 1e-6
NEG_INF = -1e30
D_IN = 3 * D_HYENA + D_HYENA + D_ATTN + 2 * KV_DIM + D_ATTN

kernel_name = "hybrid_hyena_swa_alibi_sandwich"


def rmsnorm(x, g):
    xf = x.astype(jnp.float32)
    xf = xf * lax.rsqrt(jnp.mean(xf * xf, axis=-1, keepdims=True) + RMS_EPS)
    return (xf * g.astype(jnp.float32)).astype(x.dtype)


def short_conv(u, w, b):
    L = u.shape[1]
    half = SHORT_CONV // 2
    up = jnp.pad(u, ((0, 0), (half, SHORT_CONV - 1 - half), (0, 0)))
    y = up[:, 0:L] * w[0]
    for j in range(1, SHORT_CONV):
        y = y + up[:, j:j + L] * w[j]
    return y + b


def hyena_filter(L, w_f1, b_f1, w_f2, b_f2, w_f3, b_f3, w_f4, sin_freq):
    f32 = jnp.float32
    t_np = np.arange(L, dtype=np.float32)
    t_norm_np = t_np / np.float32(max(L - 1, 1))
    w_np = np.float32(2.0 * math.pi) * t_np / np.float32(L)
    bands_np = np.linspace(1e-4, N_BANDS - 1, N_BANDS).astype(np.float32)
    ang_np = w_np[:, None] * bands_np[None, :]
    z = jnp.asarray(np.concatenate([t_norm_np[:, None], np.cos(ang_np), -np.sin(ang_np)], axis=-1))
    fr = sin_freq.astype(f32)
    h = jnp.sin(fr[0] * (z @ w_f1.astype(f32) + b_f1.astype(f32)))
    h = jnp.sin(fr[1] * (h @ w_f2.astype(f32) + b_f2.astype(f32)))
    h = jnp.sin(fr[2] * (h @ w_f3.astype(f32) + b_f3.astype(f32)))
    h = (h @ w_f4.astype(f32)).reshape(L, 2, D_HYENA)
    min_decay = math.log(DECAY_TARGET) / SLOW_DECAY_PCT
    max_decay = math.log(DECAY_TARGET) / FAST_DECAY_PCT
    deltas_np = np.abs(np.linspace(min_decay, max_decay, D_HYENA)).astype(np.float32)
    decay = jnp.asarray(np.exp(-t_norm_np[:, None] * deltas_np[None, :]).astype(np.float32))
    h = h * decay[:, None, :]
    h_fwd = h[:, 0, :]
    h_bwd = h[1:, 1, :]
    k = jnp.concatenate([h_fwd, jnp.zeros((1, D_HYENA), f32), h_bwd[::-1]], axis=0)
    return k * lax.rsqrt(jnp.sum(k * k, axis=0, keepdims=True) + 1e-12)


def hyena_mixer(u3, w_short, b_short, filt, hyena_d):
    L = u3.shape[1]
    uc = short_conv(u3, w_short, b_short)
    x0 = uc[..., :D_HYENA]
    x1 = uc[..., D_HYENA:2 * D_HYENA]
    v = uc[..., 2 * D_HYENA:]
    v = (v * x1).astype(jnp.float32)
    vf = jnp.fft.rfft(v, n=2 * L, axis=1)
    kf = jnp.fft.rfft(filt, axis=0)
    y = jnp.fft.irfft(vf * kf[None], n=2 * L, axis=1)[:, :L]
    y = (y + v * hyena_d.astype(jnp.float32)).astype(u3.dtype)
    return y * x0


def alibi_slopes_np():
    return np.exp2(-8.0 * np.arange(1, N_HEADS + 1, dtype=np.float32) / N_HEADS).astype(np.float32)


def windowed_attention(q, k, v, sink):
    B, L, _ = q.shape
    nb = L // BLOCK
    span = BLOCK + 2 * WINDOW
    scale = HEAD_DIM ** -0.5
    q5 = (q * scale).reshape(B, L, N_KV_HEADS, Q_PER_KV, HEAD_DIM)
    pad = ((0, 0), (WINDOW, WINDOW), (0, 0), (0, 0))
    k_pad = jnp.pad(k.reshape(B, L, N_KV_HEADS, HEAD_DIM), pad)
    v_pad = jnp.pad(v.reshape(B, L, N_KV_HEADS, HEAD_DIM), pad)
    slope = jnp.asarray(alibi_slopes_np().reshape(N_KV_HEADS, Q_PER_KV))
    sink_f = sink.astype(jnp.float32).reshape(N_KV_HEADS, Q_PER_KV)
    sk = sink_f[None, :, :, None]
    outs = []
    for i in range(nb):
        start = i * BLOCK
        qb = q5[:, start:start + BLOCK]
        kb = k_pad[:, start:start + span]
        vb = v_pad[:, start:start + span]
        q_pos = start + np.arange(BLOCK)
        k_pos = start - WINDOW + np.arange(span)
        rel_np = np.abs(k_pos[None, :] - q_pos[:, None])
        valid = jnp.asarray((rel_np <= WINDOW) & (k_pos >= 0)[None, :] & (k_pos < L)[None, :])
        rel = jnp.asarray(rel_np.astype(np.float32))
        s = jnp.einsum('bqkgd,bskd->bkgqs', qb, kb).astype(jnp.float32)
        s = s - slope[None, :, :, None, None] * rel
        s = jnp.where(valid, s, NEG_INF)
        m = jnp.maximum(jnp.max(s, axis=-1), sk)
        p = jnp.exp(s - m[..., None])
        den = jnp.sum(p, axis=-1) + jnp.exp(sk - m)
        o = jnp.einsum('bkgqs,bskd->bqkgd', p.astype(vb.dtype), vb)
        outs.append(o / jnp.transpose(den, (0, 3, 1, 2))[..., None].astype(o.dtype))
    out = jnp.concatenate(outs, axis=1)
    return out.reshape(B, L, D_ATTN)


def setup_inputs(seed: int = 0) -> dict:
    key = jax.random.key(seed)
    ks = jax.random.split(key, 20)
    nrm = jax.random.normal
    f32 = jnp.float32
    return {
        "x": nrm(ks[0], (BATCH, SEQ, D_MODEL), f32),
        "pre_g": 1.0 + 0.05 * nrm(ks[1], (DEPTH, D_MODEL), f32),
        "w_in": nrm(ks[2], (DEPTH, D_MODEL, D_IN), f32) * D_MODEL ** -0.5,
        "w_short": nrm(ks[3], (DEPTH, SHORT_CONV, 3 * D_HYENA), f32) * SHORT_CONV ** -0.5,
        "b_short": 0.02 * nrm(ks[4], (DEPTH, 3 * D_HYENA), f32),
        "w_f1": nrm(ks[5], (DEPTH, POS_EMB_DIM, FILTER_ORDER), f32) * POS_EMB_DIM ** -0.5,
        "b_f1": 0.1 * nrm(ks[6], (DEPTH, FILTER_ORDER), f32),
        "w_f2": nrm(ks[7], (DEPTH, FILTER_ORDER, FILTER_ORDER), f32) * FILTER_ORDER ** -0.5,
        "b_f2": 0.1 * nrm(ks[8], (DEPTH, FILTER_ORDER), f32),
        "w_f3": nrm(ks[9], (DEPTH, FILTER_ORDER, FILTER_ORDER), f32) * FILTER_ORDER ** -0.5,
        "b_f3": 0.1 * nrm(ks[10], (DEPTH, FILTER_ORDER), f32),
        "w_f4": nrm(ks[11], (DEPTH, FILTER_ORDER, 2 * D_HYENA), f32) * FILTER_ORDER ** -0.5,
        "sin_freq": 1.0 + 0.1 * nrm(ks[12], (DEPTH, 3, FILTER_ORDER), f32),
        "hyena_d": nrm(ks[13], (DEPTH, D_HYENA), f32),
        "attn_sink": 0.5 * nrm(ks[14], (DEPTH, N_HEADS), f32),
        "w_out": nrm(ks[15], (DEPTH, D_MIX, D_MODEL), f32) * D_MIX ** -0.5,
        "post_g": 1.0 + 0.05 * nrm(ks[16], (DEPTH, D_MODEL), f32),
    }


def reference(x, pre_g, w_in, w_short, b_short, w_f1, b_f1, w_f2, b_f2, w_f3, b_f3,
              w_f4, sin_freq, hyena_d, attn_sink, w_out, post_g):
    L = x.shape[1]
    o_hg = 3 * D_HYENA
    o_q = o_hg + D_HYENA
    o_k = o_q + D_ATTN
    o_v = o_k + KV_DIM
    o_ag = o_v + KV_DIM
    for l in range(DEPTH):
        h = rmsnorm(x, pre_g[l])
        z = h @ w_in[l]
        u_h, g_h = z[..., :o_hg], z[..., o_hg:o_q]
        q, k, v = z[..., o_q:o_k], z[..., o_k:o_v], z[..., o_v:o_ag]
        g_a = z[..., o_ag:]
        filt = hyena_filter(L, w_f1[l], b_f1[l], w_f2[l], b_f2[l], w_f3[l], b_f3[l],
                            w_f4[l], sin_freq[l])
        y_h = hyena_mixer(u_h, w_short[l], b_short[l], filt, hyena_d[l]) * jax.nn.silu(g_h)
        y_a = windowed_attention(q, k, v, attn_sink[l]) * jax.nn.silu(g_a)
        y = jnp.concatenate([y_h, y_a], axis=-1) @ w_out[l]
        x = x + rmsnorm(y, post_g[l])
    return x
```

```python
import math
from contextlib import ExitStack

import numpy as np
import ml_dtypes

import concourse.bass as bass
import concourse.mybir as mybir
from concourse.bass_utils import run_bass_kernel_spmd

F32 = mybir.dt.float32
BF16 = mybir.dt.bfloat16
I32 = mybir.dt.int32
AF = mybir.ActivationFunctionType
ALU = mybir.AluOpType
NPBF = ml_dtypes.bfloat16

NCORE = 8
NB = 4
L = 2048
DM = 1024
DH = 512
NCH = 26
NFFT = 4096
EPS = 1e-6

ENGS = ("tensor", "vector", "scalar", "gpsimd", "sync")


class Sched:
    def __init__(self, nc, stack):
        self.nc = nc
        self.stack = stack
        self.q = {e: [] for e in ENGS}
        self.esem = {e: stack.enter_context(nc.semaphore("s_" + e)) for e in ENGS}
        self.ecnt = {e: 0 for e in ENGS}
        self.seen = {e: {} for e in ENGS}
        self.dsem = {}
        self.lastw = {}
        self.readers = {}

    def _need(self, eng, ev, waits):
        if ev is None:
            return
        sem, val = ev
        if eng == "tensor" and sem is self.esem["tensor"]:
            return
        k = id(sem)
        if self.seen[eng].get(k, 0) >= val:
            return
        cur = waits.get(k)
        if cur is None or cur[1] < val:
            waits[k] = (sem, val)

    def _deps(self, eng, reads, writes):
        waits = {}
        for k in reads:
            self._need(eng, self.lastw.get(k), waits)
        for k in writes:
            self._need(eng, self.lastw.get(k), waits)
            for ev in self.readers.get(k, ()):
                self._need(eng, ev, waits)
        for k, (sem, val) in waits.items():
            self.seen[eng][k] = val
            self.q[eng].append(lambda e, sem=sem, val=val: e.wait_ge(sem, val))

    def _commit(self, ev, reads, writes):
        for k in reads:
            self.readers.setdefault(k, []).append(ev)
        for k in writes:
            self.lastw[k] = ev
            self.readers[k] = []

    def op(self, eng, fn, reads=(), writes=()):
        self._deps(eng, reads, writes)
        self.ecnt[eng] += 1
        sem, val = self.esem[eng], self.ecnt[eng]
        self.q[eng].append(lambda e, fn=fn, sem=sem: fn(e).then_inc(sem, 1))
        self._commit((sem, val), reads, writes)

    def mm(self, fns, reads=(), writes=()):
        eng = "tensor"
        self._deps(eng, reads, writes)
        for fn in fns[:-1]:
            self.q[eng].append(lambda e, fn=fn: fn(e))
        self.ecnt[eng] += 1
        sem, val = self.esem[eng], self.ecnt[eng]
        fn = fns[-1]
        self.q[eng].append(lambda e, fn=fn, sem=sem: fn(e).then_inc(sem, 1))
        self._commit((sem, val), reads, writes)

    def dma(self, eng, out, in_, semkey, reads=(), writes=()):
        self._deps(eng, reads, writes)
        if semkey not in self.dsem:
            self.dsem[semkey] = [self.stack.enter_context(self.nc.semaphore("d_%d" % len(self.dsem))), 0]
        ent = self.dsem[semkey]
        ent[1] += 16
        sem, val = ent[0], ent[1]
        self.q[eng].append(lambda e, out=out, in_=in_, sem=sem: e.dma_start(out=out, in_=in_).then_inc(sem, 16))
        self._commit((sem, val), reads, writes)

    def wait_all_dma(self, eng, keys):
        for k in keys:
            sem, val = self.dsem[k]
            if self.seen[eng].get(id(sem), 0) < val:
                self.seen[eng][id(sem)] = val
                self.q[eng].append(lambda e, sem=sem, val=val: e.wait_ge(sem, val))

    def barrier(self):
        for eng in ENGS:
            for o in ENGS:
                if self.ecnt[o] == 0 or (o == eng == "tensor"):
                    continue
                sem, val = self.esem[o], self.ecnt[o]
                if self.seen[eng].get(id(sem), 0) < val:
                    self.seen[eng][id(sem)] = val
                    self.q[eng].append(lambda e, sem=sem, val=val: e.wait_ge(sem, val))
            for k, (sem, val) in self.dsem.items():
                if val and self.seen[eng].get(id(sem), 0) < val:
                    self.seen[eng][id(sem)] = val
                    self.q[eng].append(lambda e, sem=sem, val=val: e.wait_ge(sem, val))

    def emit(self):
        with self.nc.Block() as block:
            for name in ENGS:
                q = self.q[name]
                if not q:
                    continue

                def body(e, q=q):
                    for th in q:
                        th(e)
                getattr(block, name)(body)


_CONST = {}


def consts():
    if _CONST:
        return _CONST
    n1 = np.arange(128, dtype=np.float64)
    n2 = np.arange(16, dtype=np.float64)
    k1 = np.arange(128, dtype=np.float64)
    th = 2 * np.pi * (16 * n1[:, None, None] + n2[None, :, None]) * (k1[None, None, :] + 0.5) / NFFT
    fw = np.stack([np.cos(th), -np.sin(th)], axis=2)
    gw = (2.0 / NFFT) * np.transpose(fw, (3, 1, 2, 0))
    ph = 2 * np.pi * np.outer(n2, n2) / 16
    eye8 = np.eye(8)
    kC = np.kron(np.cos(ph), eye8)
    kS = np.kron(np.sin(ph), eye8)
    kmats = np.stack([kC, kS, -kS, kC, np.eye(128), np.ones((128, 128))], axis=1)
    _CONST["fw"] = np.ascontiguousarray(fw).astype(NPBF)
    _CONST["gw"] = np.ascontiguousarray(gw).astype(NPBF)
    _CONST["kmats"] = np.ascontiguousarray(kmats).astype(NPBF)
    slopes = np.exp2(-8.0 * np.arange(1, 9, dtype=np.float64) / 8).reshape(2, 4)
    pk = np.arange(128)[:, None]
    pq = np.arange(128)[None, :]
    dt = np.zeros((128, 2, 3, 4, 128))
    for s in range(3):
        rel = np.abs((s - 1) * 128 + pk - pq)
        for kv in range(2):
            for g in range(4):
                dt[:, kv, s, g, :] = np.where(rel <= 128, np.exp(-slopes[kv, g] * rel), 0.0)
    _CONST["dtab"] = dt.astype(NPBF)
    t = np.arange(L, dtype=np.float32)
    tn = t / np.float32(L - 1)
    w = np.float32(2.0 * math.pi) * t / np.float32(L)
    bands = np.linspace(1e-4, 15, 16).astype(np.float32)
    ang = w[:, None] * bands[None, :]
    z = np.concatenate([tn[:, None], np.cos(ang), -np.sin(ang)], axis=-1).astype(np.float32)
    _CONST["zT"] = np.ascontiguousarray(z.T)
    min_decay = math.log(1e-2) / 1.5
    max_decay = math.log(1e-2) / 0.3
    deltas = np.abs(np.linspace(min_decay, max_decay, DH)).astype(np.float32)
    decay = np.exp(-tn[:, None] * deltas[None, :]).astype(np.float32)
    _CONST["decay"] = np.ascontiguousarray(decay.T.reshape(4, 128, L).transpose(1, 0, 2))
    return _CONST


def build(nb=NB, dbg=(), stop=None):
    nc = bass.Bass("TRN2", target_bir_lowering=False)

    def din(name, shape, dt=F32):
        return nc.dram_tensor(name, list(shape), dt, kind="ExternalInput").ap()

    x_d = din("x", [nb, L, DM])
    win_d = din("win", [NCH, 128, 8, 128])
    wout_d = din("wout", [128, 8, DM])
    preg_d = din("preg", [128, 8])
    postg_d = din("postg", [128, DM])
    convw_d = din("convw", [128, 12, 4])
    hd_d = din("hd", [128, 4])
    sink_d = din("sink", [128, 8])
    wf1_d = din("wf1", [33, 64])
    wf2_d = din("wf2", [64, 64])
    wf3_d = din("wf3", [64, 64])
    wf4_d = din("wf4", [64, 1024])
    fb_d = din("fb", [64, 3])
    fr_d = din("fr", [64, 3])
    zT_d = din("zT", [33, L])
    decay_d = din("decay", [128, 4, L])
    fw_d = din("fw", [128, 16, 2, 128], BF16)
    gw_d = din("gw", [128, 16, 2, 128], BF16)
    kmats_d = din("kmats", [128, 6, 128], BF16)
    dtab_d = din("dtab", [128, 2, 3, 4, 128], BF16)
    out_d = nc.dram_tensor("out", [nb, L, DM], F32, kind="ExternalOutput").ap()
    kfs_d = nc.dram_tensor("kfs", [4, 128, 16 * 2 * 128], BF16).ap()
    wsc_d = nc.dram_tensor("wsc", [NCH, 128, 8 * 128], BF16).ap()
    dbg_out = {}

    with ExitStack() as st:
        S = Sched(nc, st)

        def sb(name, shape, dt):
            return st.enter_context(nc.sbuf_tensor("sb_" + name, list(shape), dt))

        kmats = sb("kmats", [128, 6, 128], BF16)
        fw = sb("fw", [128, 16, 2, 128], BF16)
        gw = sb("gw", [128, 16, 2, 128], BF16)
        dtab = sb("dtab", [128, 2, 3, 4, 128], BF16)
        esinkT = sb("esinkT", [128, 8, 128], BF16)
        woutb = sb("woutb", [128, 8, DM], BF16)
        postg = sb("postg", [128, DM], F32)
        preg = sb("preg", [128, 8], F32)
        convw = sb("convw", [128, 12, 4], F32)
        hd = sb("hd", [128, 4], F32)
        sinkt = sb("sinkt", [128, 8], F32)
        small = sb("small", [128, 64], F32)
        kfslot = sb("kfslot", [128, 16, 2, 128], BF16)
        wst = sb("wst", [128, 2, 8, 128], F32)
        wbf = sb("wbf", [128, 2, 8, 128], BF16)
        xs = sb("xs", [128, 4, DM], F32)
        xn = sb("xn", [128, DM], BF16)
        junk = sb("junk", [128, DM], BF16)
        outt = sb("outt", [128, 2, DM], F32)
        ss = sb("ss", [128, 32], F32)
        rstd = sb("rstd", [128, 32], F32)
        hT = sb("hT", [128, 8, L], BF16)
        ymixT = sb("ymixT", [128, 8, L], BF16)
        REG_BF = 22 * 1024
        reg = sb("reg", [128, REG_BF], BF16)

        def carve_bf(off_kb, shape):
            n = int(np.prod(shape[1:]))
            o = off_kb * 512
            ap = reg[:, o:o + n]
            if len(shape) == 3:
                ap = ap.rearrange("p (a b) -> p a b", b=shape[2])
            elif len(shape) == 4:
                ap = ap.rearrange("p (a b c) -> p a b c", b=shape[2], c=shape[3])
            return ap

        def carve_f32(off_kb, shape):
            n = int(np.prod(shape[1:]))
            o = off_kb * 512
            ap = reg[:, o:o + 2 * n].bitcast(F32)
            if len(shape) == 3:
                ap = ap.rearrange("p (a b) -> p a b", b=shape[2])
            return ap

        x0c = carve_bf(0, [128, L])
        x1c = carve_bf(4, [128, L])
        vc = carve_bf(8, [128, L])
        sg = carve_bf(12, [128, L])
        vT = carve_bf(16, [128, 16, 128])
        A_sb = carve_bf(20, [128, 2, 16, 128])
        A2 = carve_bf(28, [128, 2, 16, 128])
        t1 = carve_bf(36, [128, 1024])
        t2 = carve_bf(38, [128, 1024])
        x0c2 = carve_bf(40, [128, L])
        x0bufs = [x0c, x0c2]
        ytok = vT
        Bbig = A_sb
        Zt = A2
        qT = carve_bf(0, [128, 16, 512])
        kT = carve_bf(16, [128, L])
        vTa = carve_bf(20, [128, L])
        vtok = carve_bf(24, [128, 16, 128])
        Pt = carve_bf(28, [128, 2, 3, 512])
        rden = carve_f32(34, [128, 2, 512])

        ps = st.enter_context(nc.psum_tensor("ps", [128, 4096], F32))
        psb = ps[:, :].bitcast(BF16)

        def bank(i, n=1):
            return ps[:, i * 512:(i + n) * 512]

        def bankb(i, n=1):
            return psb[:, i * 1024:(i + n) * 1024]

        def B(*idx):
            return [("B", i) for i in idx]

        def dump(name, ap, keys):
            if name not in dbg:
                return
            d = nc.dram_tensor("dbg_" + name, list(ap.shape), ap.dtype, kind="ExternalOutput").ap()
            dbg_out[name] = d
            S.dma("sync", d, ap, "dbg", reads=keys)

        ident = kmats[:, 4, :]
        ones = kmats[:, 5, :]
        kC = kmats[:, 0, :]
        kS = kmats[:, 1, :]
        kSn = kmats[:, 2, :]
        kCS = kmats[:, 0:2, :].rearrange("p a b -> p (a b)")
        kSnC = kmats[:, 2:4, :].rearrange("p a b -> p (a b)")

        def cload(t_ap, d_ap, key):
            S.dma("sync", t_ap, d_ap, "const", writes=[key])

        cload(kmats[:], kmats_d, "kmats")
        cload(fw[:], fw_d, "fw")
        cload(gw[:], gw_d, "gw")
        cload(dtab[:], dtab_d, "dtab")
        cload(postg[:], postg_d, "postg")
        cload(preg[:], preg_d, "preg")
        cload(convw[:], convw_d, "convw")
        cload(hd[:], hd_d, "hd")
        cload(sinkt[:], sink_d, "sinkt")

        for ch in range(8):
            sl = ch % 2
            S.dma("sync", xs[:, sl, :], wout_d[:, ch, :], ("xs", sl), writes=[("xs", sl)])
            S.op("scalar", lambda e, ch=ch, sl=sl: e.activation(out=woutb[:, ch, :], in_=xs[:, sl, :], func=AF.Copy, scale=0.5),
                 reads=[("xs", sl)], writes=["woutb"])

        S.op("scalar", lambda e: e.activation(out=small[:, 0:8], in_=sinkt[:], func=AF.Exp), reads=["sinkt"], writes=["small_es"])
        S.op("vector", lambda e: e.tensor_scalar_mul(out=small[:, 0:8], in0=small[:, 0:8], scalar1=1.0 / 128),
             reads=["small_es"], writes=["small_es"])
        S.op("vector", lambda e: e.tensor_copy(out=esinkT[:], in_=small[:, 0:8].unsqueeze(2).to_broadcast([128, 8, 128])),
             reads=["small_es"], writes=["esinkT"])

        def fft_forward(src, src_keys, on_half):
            fns = []
            for a in range(16):
                fns.append(lambda e, a=a: e.transpose(out=bankb(4, 2)[:, a * 128:(a + 1) * 128], in_=src[:, a:L:16], identity=ident))
            S.mm(fns, reads=list(src_keys) + ["kmats"], writes=B(4, 5))
            S.op("scalar", lambda e: e.activation(out=vT.rearrange("p a c -> p (a c)"), in_=bankb(4, 2), func=AF.Copy),
                 reads=B(4, 5), writes=["vT"])
            yield
            for ri in range(2):
                bk = 4
                fns = []
                for a in range(16):
                    fns.append(lambda e, a=a, ri=ri, bk=bk: e.matmul(out=bank(bk, 4)[:, a * 128:(a + 1) * 128], lhsT=fw[:, a, ri, :],
                                                                      rhs=vT[:, a, :], start=True, stop=True))
                S.mm(fns, reads=["vT", "fw"], writes=B(bk, bk + 1, bk + 2, bk + 3))
                srcv = bank(bk, 4).rearrange("p (a g c) -> p a g c", g=16, c=8)
                dstv = A_sb[:, ri, :, :].rearrange("p g (a c) -> p a g c", c=8)
                if ri == 0:
                    S.op("scalar", lambda e, srcv=srcv, dstv=dstv: e.activation(out=dstv, in_=srcv, func=AF.Copy),
                         reads=B(bk, bk + 1, bk + 2, bk + 3), writes=[("A_sb", ri)])
                else:
                    S.op("vector", lambda e, srcv=srcv, dstv=dstv: e.tensor_copy(out=dstv, in_=srcv),
                         reads=B(bk, bk + 1, bk + 2, bk + 3), writes=[("A_sb", ri)])
                yield
            fns = []
            for ri in range(2):
                for g in range(16):
                    o = (ri * 16 + g) * 128
                    fns.append(lambda e, g=g, ri=ri, o=o: e.transpose(out=bankb(4, 4)[:, o:o + 128],
                                                                       in_=A_sb[:, ri, g, :], identity=ident))
            S.mm(fns, reads=[("A_sb", 0), ("A_sb", 1), "kmats"], writes=B(4, 5, 6, 7))
            pv = bankb(4, 4).rearrange("p (r g k) -> p r g k", r=2, k=128)
            S.op("scalar", lambda e: e.activation(out=A2[:, :, 0:8, :], in_=pv[:, :, 0:8, :], func=AF.Copy),
                 reads=B(4, 5, 6, 7), writes=[("A2", 0)])
            S.op("vector", lambda e: e.tensor_copy(out=A2[:, :, 8:16, :], in_=pv[:, :, 8:16, :]),
                 reads=B(4, 5, 6, 7), writes=[("A2", 1)])
            yield
            for h in range(2):
                fns = []
                for q4 in range(2):
                    g0 = h * 8 + q4 * 4
                    ar = A2[:, 0, g0:g0 + 4, :].rearrange("p g k -> p (g k)")
                    ai = A2[:, 1, g0:g0 + 4, :].rearrange("p g k -> p (g k)")
                    xr = bank(4 + q4)
                    xi = bank(6 + q4)
                    fns.append(lambda e, ar=ar, xr=xr: e.matmul(out=xr, lhsT=kC, rhs=ar, start=True, stop=False))
                    fns.append(lambda e, ai=ai, xr=xr: e.matmul(out=xr, lhsT=kS, rhs=ai, start=False, stop=True))
                    fns.append(lambda e, ai=ai, xi=xi: e.matmul(out=xi, lhsT=kC, rhs=ai, start=True, stop=False))
                    fns.append(lambda e, ar=ar, xi=xi: e.matmul(out=xi, lhsT=kSn, rhs=ar, start=False, stop=True))
                S.mm(fns, reads=[("A2", h), "kmats"], writes=B(4, 5, 6, 7))
                on_half(h)
                yield

        def run_gen(g):
            for _ in g:
                pass

        def interleave(ga, gb):
            alive = [ga, gb]
            while alive:
                for g in list(alive):
                    try:
                        next(g)
                    except StopIteration:
                        alive.remove(g)

        XR = bank(4, 2).rearrange("p (g k) -> p g k", k=128)
        XI = bank(6, 2).rearrange("p (g k) -> p g k", k=128)

        hTf = hT[:, :, :].rearrange("p a b -> p (a b)")
        ymf = ymixT[:, :, :].rearrange("p a b -> p (a b)")

        def f32view(flat, off_kb, n):
            return flat[:, off_kb * 512: off_kb * 512 + 2 * n].bitcast(F32)

        zT = f32view(hTf, 0, L)
        hA = f32view(hTf, 8, L)
        hB = f32view(hTf, 16, L)
        tmpf = f32view(hTf, 24, L)
        kfr = f32view(ymf, 0, L)
        kbr = f32view(ymf, 8, L)
        dec = f32view(ymf, 16, L)
        wf4 = f32view(ymf, 24, 1024)
        wsm = f32view(ymf, 28, 256)
        fbr = small[:, 8:11]
        frr = small[:, 11:14]
        frb = small[:, 14:17]
        Hfw = outt[:, 0, :].bitcast(BF16).rearrange("p (g r k) -> p g r k", r=2, k=128)
        Hfw2 = outt[:, 1, :].bitcast(BF16).rearrange("p (g r k) -> p g r k", r=2, k=128)

        S.barrier()
        S.dma("sync", zT[0:33, :], zT_d, "const", writes=["zT"])
        S.dma("sync", wsm[0:33, 0:64], wf1_d, "const", writes=["wsm"])
        S.dma("sync", wsm[0:64, 64:128], wf2_d, "const", writes=["wsm"])
        S.dma("sync", wsm[0:64, 128:192], wf3_d, "const", writes=["wsm"])
        S.dma("sync", wf4[0:64, :], wf4_d, "const", writes=["wf4"])
        S.dma("sync", fbr[0:64, :], fb_d, "const", writes=["fbr"])
        S.dma("sync", frr[0:64, :], fr_d, "const", writes=["frr"])
        S.op("vector", lambda e: e.tensor_tensor(out=frb[0:64, :], in0=frr[0:64, :], in1=fbr[0:64, :], op=ALU.mult),
             reads=["fbr", "frr"], writes=["frb"])

        S.op("vector", lambda e: e.tensor_scalar_mul(out=frb[0:64, :], in0=frb[0:64, :], scalar1=1.0 / 9), reads=["frb"], writes=["frb"])
        S.op("vector", lambda e: e.tensor_scalar_mul(out=frr[0:64, :], in0=frr[0:64, :], scalar1=1.0 / 9), reads=["frr"], writes=["frr"])

        def sin_layer(li, lhsT, kdim, src, dst):
            for nchk in range(4):
                S.mm([lambda e, nchk=nchk: e.matmul(out=bank(nchk)[0:64, :], lhsT=lhsT, rhs=src[0:kdim, nchk * 512:(nchk + 1) * 512],
                                                     start=True, stop=True)],
                     reads=["wsm", ("h", li)], writes=B(nchk))
            S.op("scalar", lambda e: e.activation(out=dst[0:64, :], in_=bank(0, 4)[0:64, :], func=AF.Sin,
                                                   scale=frr[0:64, li:li + 1], bias=frb[0:64, li:li + 1]),
                 reads=B(0, 1, 2, 3) + ["frr", "frb"], writes=[("h", li + 1)])
            for _ in range(2):
                S.op("vector", lambda e: e.tensor_tensor(out=tmpf[0:64, :], in0=dst[0:64, :], in1=dst[0:64, :], op=ALU.mult),
                     reads=[("h", li + 1)], writes=["tmpf"])
                S.op("vector", lambda e: e.tensor_scalar(out=tmpf[0:64, :], in0=tmpf[0:64, :], scalar1=-4.0, scalar2=3.0,
                                                         op0=ALU.mult, op1=ALU.add), reads=["tmpf"], writes=["tmpf"])
                S.op("vector", lambda e: e.tensor_tensor(out=dst[0:64, :], in0=dst[0:64, :], in1=tmpf[0:64, :], op=ALU.mult),
                     reads=["tmpf", ("h", li + 1)], writes=[("h", li + 1)])

        S.lastw[("h", 0)] = S.lastw.get("zT")
        sin_layer(0, wsm[0:33, 0:64], 33, zT, hA)
        sin_layer(1, wsm[0:64, 64:128], 64, hA, hB)
        sin_layer(2, wsm[0:64, 128:192], 64, hB, hA)
        h3 = hA
        dump("h3", h3[0:64, :], [("h", 3)])

        def precast(chs):
            for ch in chs:
                sl = ch % 2
                S.dma("sync", wst[:, sl, :, :], win_d[ch], ("wst", sl), writes=[("wst", sl)])
                S.op("vector", lambda e, sl=sl: e.tensor_tensor(out=wbf[:, sl, :, :], in0=wst[:, sl, :, :],
                                                                 in1=preg[:, :].unsqueeze(2).to_broadcast([128, 8, 128]), op=ALU.mult),
                     reads=[("wst", sl), "preg"], writes=[("wbf", sl)])
                S.dma("sync", wsc_d[ch], wbf[:, sl, :, :].rearrange("p a b -> p (a b)"), ("wsc_st", sl), reads=[("wbf", sl)], writes=[("wsc", ch)])

        for cc in range(4):
            precast(range(cc * 7, min(NCH, cc * 7 + 7)))
            S.dma("sync", dec[:, :], decay_d[:, cc, :], "dec", writes=["dec"])
            for fb_i, dstk in ((0, kfr), (1, kbr)):
                col = fb_i * 512 + cc * 128
                for nchk in range(4):
                    S.mm([lambda e, nchk=nchk, col=col: e.matmul(out=bank(nchk), lhsT=wf4[0:64, col:col + 128],
                                                                  rhs=h3[0:64, nchk * 512:(nchk + 1) * 512], start=True, stop=True)],
                         reads=["wf4", ("h", 3)], writes=B(nchk))
                S.op("vector", lambda e, dstk=dstk: e.tensor_tensor(out=dstk[:, :], in0=bank(0, 4), in1=dec[:, :], op=ALU.mult),
                     reads=B(0, 1, 2, 3) + ["dec"], writes=[("kraw", fb_i)])
            S.op("gpsimd", lambda e: e.memset(kbr[:, 0:1], 0.0), reads=[], writes=[("kraw", 1)])
            S.op("scalar", lambda e: e.activation(out=tmpf[:, :], in_=kfr[:, :], func=AF.Square), reads=[("kraw", 0)], writes=["tmpf"])
            S.op("vector", lambda e: e.reduce_sum(out=small[:, 20:21], in_=tmpf[:, :], axis=mybir.AxisListType.X), reads=["tmpf"], writes=["ssq0"])
            S.op("scalar", lambda e: e.activation(out=tmpf[:, :], in_=kbr[:, :], func=AF.Square), reads=[("kraw", 1)], writes=["tmpf"])
            S.op("vector", lambda e: e.reduce_sum(out=small[:, 21:22], in_=tmpf[:, :], axis=mybir.AxisListType.X), reads=["tmpf"], writes=["ssq1"])
            S.op("vector", lambda e: e.scalar_tensor_tensor(out=small[:, 22:23], in0=small[:, 20:21], scalar=1e-12, in1=small[:, 21:22],
                                                            op0=ALU.add, op1=ALU.add), reads=["ssq0", "ssq1"], writes=["ssq"])
            S.op("scalar", lambda e: e.activation(out=small[:, 22:23], in_=small[:, 22:23], func=AF.Ln), reads=["ssq"], writes=["ssq"])
            S.op("scalar", lambda e: e.activation(out=small[:, 23:24], in_=small[:, 22:23], func=AF.Exp, scale=-0.5), reads=["ssq"], writes=["krs"])
            S.op("vector", lambda e: e.tensor_scalar(out=x0c, in0=kfr[:, :], scalar1=small[:, 23:24], scalar2=None, op0=ALU.mult),
                 reads=[("kraw", 0), "krs"], writes=["x0c"])
            S.op("vector", lambda e, cc=cc: e.scalar_tensor_tensor(out=x0c[:, 0:1], in0=kfr[:, 0:1], scalar=small[:, 23:24], in1=hd[:, cc:cc + 1],
                                                                    op0=ALU.mult, op1=ALU.add), reads=[("kraw", 0), "krs", "hd", "x0c"], writes=["x0c"])
            S.op("vector", lambda e: e.tensor_scalar(out=x1c, in0=kbr[:, :], scalar1=small[:, 23:24], scalar2=None, op0=ALU.mult),
                 reads=[("kraw", 1), "krs"], writes=["x1c"])
            if cc == 0:
                dump("kf0", x0c, ["x0c"])
                dump("kb0", x1c, ["x1c"])

            def keep_f(h):
                dst = Hfw if h == 0 else Hfw2
                S.op("scalar", lambda e, dst=dst: e.activation(out=dst[:, :, 0, :], in_=XR, func=AF.Copy), reads=B(4, 5), writes=[("Hf", h)])
                S.op("vector", lambda e, dst=dst: e.tensor_copy(out=dst[:, :, 1, :], in_=XI), reads=B(6, 7), writes=[("Hf", h)])

            def keep_b(h):
                dst = Hfw if h == 0 else Hfw2
                S.op("vector", lambda e, dst=dst, h=h: e.tensor_tensor(out=kfslot[:, h * 8:(h + 1) * 8, 0, :], in0=XR, in1=dst[:, :, 0, :], op=ALU.add),
                     reads=B(4, 5) + [("Hf", h)], writes=[("kfslot", h)])
                S.op("vector", lambda e, dst=dst, h=h: e.tensor_tensor(out=kfslot[:, h * 8:(h + 1) * 8, 1, :], in0=dst[:, :, 1, :], in1=XI, op=ALU.subtract),
                     reads=B(6, 7) + [("Hf", h)], writes=[("kfslot", h)])

            run_gen(fft_forward(x0c, ["x0c"], keep_f))
            run_gen(fft_forward(x1c, ["x1c"], keep_b))
            S.dma("sync", kfs_d[cc], kfslot[:, :, :, :].rearrange("p g r k -> p (g r k)"), ("kfs", cc),
                  reads=[("kfslot", 0), ("kfslot", 1)], writes=[("kfs", cc)])
            if cc == 0:
                dump("Kf0", kfslot[:, :, :, :], [("kfslot", 0), ("kfslot", 1)])

        S.barrier()

        wcount = [0]

        def load_w(ch):
            sl = wcount[0] % 2
            wcount[0] += 1
            S.dma("sync", wbf[:, sl, :, :].rearrange("p a b -> p (a b)"), wsc_d[ch], ("wbf", sl), reads=[("wsc", ch)], writes=[("wbf", sl)])
            return sl

        ucount = [0]

        def inproj_gen(ch, bk):
            sl = load_w(ch)
            for tcn in range(4):
                fns = []
                for kc in range(8):
                    fns.append(lambda e, kc=kc, tcn=tcn, sl=sl, bk=bk: e.matmul(out=bank(bk + tcn), lhsT=wbf[:, sl, kc, :],
                                                                                  rhs=hT[:, kc, tcn * 512:(tcn + 1) * 512],
                                                                                  start=(kc == 0), stop=(kc == 7)))
                S.mm(fns, reads=[("wbf", sl), "hT"], writes=B(bk + tcn))
                yield

        def inproj(ch):
            bk = 4 * (ucount[0] % 2)
            ucount[0] += 1
            run_gen(inproj_gen(ch, bk))
            return bk

        def conv_evac(bk, dst, dkey, ci):
            U = bank(bk, 4)
            ks = B(bk, bk + 1, bk + 2, bk + 3)
            S.op("scalar", lambda e: e.activation(out=dst, in_=U, func=AF.Identity, scale=convw[:, ci, 1:2], bias=convw[:, ci, 3:4]),
                 reads=ks + ["convw"], writes=[dkey])
            S.op("vector", lambda e: e.scalar_tensor_tensor(out=dst[:, 1:L], in0=U[:, 0:L - 1], scalar=convw[:, ci, 0:1], in1=dst[:, 1:L],
                                                            op0=ALU.mult, op1=ALU.add), reads=ks + ["convw", dkey], writes=[dkey])
            S.op("vector", lambda e: e.scalar_tensor_tensor(out=dst[:, 0:L - 1], in0=U[:, 1:L], scalar=convw[:, ci, 2:3], in1=dst[:, 0:L - 1],
                                                            op0=ALU.mult, op1=ALU.add), reads=ks + ["convw", dkey], writes=[dkey])

        def silu_evac(bk, dst, dkey):
            U = bank(bk, 4)
            ks = B(bk, bk + 1, bk + 2, bk + 3)
            S.op("scalar", lambda e: e.activation(out=dst, in_=U, func=AF.Tanh, scale=0.5), reads=ks, writes=[dkey])
            S.op("vector", lambda e: e.scalar_tensor_tensor(out=dst, in0=dst, scalar=1.0, in1=U, op0=ALU.add, op1=ALU.mult),
                 reads=ks + [dkey], writes=[dkey])

        junk2 = sb("junk2", [128, DM], BF16)

        def phase1_tile(b, tt, sl):
            S.dma("sync", xs[:, sl, :], x_d[b, tt * 128:(tt + 1) * 128, :], ("xs", sl), writes=[("xs", sl)])
            S.op("scalar", lambda e, sl=sl, tt=tt: e.activation(out=junk[:, :], in_=xs[:, sl, :], func=AF.Square, scale=1.0 / 32),
                 reads=[("xs", sl)], writes=["junk"])
            S.op("vector", lambda e, tt=tt: e.reduce_sum(out=ss[:, tt:tt + 1], in_=junk[:, :], axis=mybir.AxisListType.X),
                 reads=["junk"], writes=[("ss", tt)])
            S.op("vector", lambda e, tt=tt: e.tensor_scalar_add(out=ss[:, tt:tt + 1], in0=ss[:, tt:tt + 1], scalar1=EPS),
                 reads=[("ss", tt)], writes=[("ss", tt)])
            S.op("scalar", lambda e, tt=tt: e.activation(out=rstd[:, tt:tt + 1], in_=ss[:, tt:tt + 1], func=AF.Ln),
                 reads=[("ss", tt)], writes=[("rstd", tt)])
            S.op("scalar", lambda e, tt=tt: e.activation(out=rstd[:, tt:tt + 1], in_=rstd[:, tt:tt + 1], func=AF.Exp, scale=-0.5),
                 reads=[("rstd", tt)], writes=[("rstd", tt)])
            S.op("vector", lambda e, sl=sl, tt=tt: e.tensor_scalar(out=xn[:, :], in0=xs[:, sl, :], scalar1=rstd[:, tt:tt + 1], scalar2=None,
                                                                   op0=ALU.mult), reads=[("xs", sl), ("rstd", tt)], writes=["xn"])
            fns = []
            for kc in range(8):
                fns.append(lambda e, kc=kc: e.transpose(out=bankb(3)[:, kc * 128:(kc + 1) * 128], in_=xn[:, kc * 128:(kc + 1) * 128], identity=ident))
            S.mm(fns, reads=["xn", "kmats"], writes=B(3))
            S.op("vector", lambda e, tt=tt: e.tensor_copy(out=hT[:, :, tt * 128:(tt + 1) * 128],
                                                          in_=bankb(3).rearrange("p (a b) -> p a b", b=128)), reads=B(3), writes=["hT"])

        ymk = [("ymixT", c) for c in (0, 1, 2, 3, "a")]

        def phase5_tile(b, tt, sl, yb):
            S.dma("sync", xs[:, sl, :], x_d[b, tt * 128:(tt + 1) * 128, :], ("xs", sl), writes=[("xs", sl)])
            for hf in range(2):
                fns = []
                for ch in range(8):
                    fns.append(lambda e, ch=ch, hf=hf, tt=tt, yb=yb: e.matmul(out=bank(yb + hf), lhsT=ymixT[:, ch, tt * 128:(tt + 1) * 128],
                                                                                rhs=woutb[:, ch, hf * 512:(hf + 1) * 512],
                                                                                start=(ch == 0), stop=(ch == 7)))
                S.mm(fns, reads=ymk + ["woutb"], writes=B(yb + hf))
            yk = B(yb, yb + 1)
            c0 = 16 + tt
            osl = tt % 2
            S.op("scalar", lambda e, yb=yb, c0=c0: e.activation(out=junk2[:, :], in_=bank(yb, 2), func=AF.Square, scale=1.0 / 32),
                 reads=yk, writes=["junk2"])
            S.op("vector", lambda e, c0=c0: e.reduce_sum(out=ss[:, c0:c0 + 1], in_=junk2[:, :], axis=mybir.AxisListType.X),
                 reads=["junk2"], writes=[("ss", c0)])
            S.op("vector", lambda e, c0=c0: e.tensor_scalar_add(out=ss[:, c0:c0 + 1], in0=ss[:, c0:c0 + 1], scalar1=EPS),
                 reads=[("ss", c0)], writes=[("ss", c0)])
            S.op("scalar", lambda e, c0=c0: e.activation(out=rstd[:, c0:c0 + 1], in_=ss[:, c0:c0 + 1], func=AF.Ln),
                 reads=[("ss", c0)], writes=[("rstd", c0)])
            S.op("scalar", lambda e, c0=c0: e.activation(out=rstd[:, c0:c0 + 1], in_=rstd[:, c0:c0 + 1], func=AF.Exp, scale=-0.5),
                 reads=[("rstd", c0)], writes=[("rstd", c0)])
            S.op("vector", lambda e, yb=yb, c0=c0, osl=osl: e.scalar_tensor_tensor(out=outt[:, osl, :], in0=bank(yb, 2), scalar=rstd[:, c0:c0 + 1],
                                                                                   in1=postg[:, :], op0=ALU.mult, op1=ALU.mult),
                 reads=yk + [("rstd", c0), "postg"], writes=[("outt", osl)])
            S.op("gpsimd", lambda e, sl=sl, osl=osl: e.tensor_tensor(out=outt[:, osl, :], in0=outt[:, osl, :], in1=xs[:, sl, :], op=ALU.add),
                 reads=[("outt", osl), ("xs", sl)], writes=[("outt", osl)])
            S.dma("sync", out_d[b, tt * 128:(tt + 1) * 128, :], outt[:, osl, :], ("outst", osl), reads=[("outt", osl)])

        if stop != "setup":
            for tt in range(16):
                phase1_tile(0, tt, tt % 2)
        for b in range(nb if stop != "setup" else 0):
            if b == 0:
                dump("hT", hT[:, :, :], ["hT"])

            if stop == "phase1":
                continue
            def hy_inproj(cc):
                base = cc * 4
                x0b = x0bufs[cc % 2]
                x0k = ("x0c", cc % 2)
                yield from inproj_gen(base + 1, 0)
                conv_evac(0, x1c, "x1c", 1 * 4 + cc)
                yield from inproj_gen(base + 2, 0)
                conv_evac(0, vc, "vc", 2 * 4 + cc)
                S.op("gpsimd", lambda e: e.tensor_tensor(out=vc, in0=vc, in1=x1c, op=ALU.mult), reads=["vc", "x1c"], writes=["vc"])
                yield from inproj_gen(base + 3, 0)
                silu_evac(0, sg, "sg")
                yield from inproj_gen(base + 0, 0)
                conv_evac(0, x0b, x0k, 0 * 4 + cc)
                S.op("gpsimd", lambda e, x0b=x0b: e.tensor_tensor(out=x0b, in0=x0b, in1=sg, op=ALU.mult), reads=[x0k, "sg"], writes=[x0k])

            def hy_fft(cc):
                x0b = x0bufs[cc % 2]
                x0k = ("x0c", cc % 2)
                S.dma("sync", kfslot[:, :, :, :].rearrange("p g r k -> p (g r k)"), kfs_d[cc], "kfslot_ld",
                      reads=[("kfs", cc)], writes=[("kfslot", 0), ("kfslot", 1)])

                def mulk(h):
                    kr = kfslot[:, h * 8:(h + 1) * 8, 0, :]
                    ki = kfslot[:, h * 8:(h + 1) * 8, 1, :]
                    t1v = t1.rearrange("p (g k) -> p g k", k=128)
                    t2v = t2.rearrange("p (g k) -> p g k", k=128)
                    zr = Zt[:, 0, h * 8:(h + 1) * 8, :]
                    zi = Zt[:, 1, h * 8:(h + 1) * 8, :]
                    kk = [("kfslot", h)]
                    S.op("vector", lambda e: e.tensor_tensor(out=t1v, in0=XR, in1=kr, op=ALU.mult), reads=B(4, 5) + kk, writes=["t1"])
                    S.op("vector", lambda e: e.tensor_tensor(out=t2v, in0=XI, in1=ki, op=ALU.mult), reads=B(6, 7) + kk, writes=["t2"])
                    S.op("gpsimd", lambda e: e.tensor_tensor(out=zr, in0=t1v, in1=t2v, op=ALU.subtract), reads=["t1", "t2", ("A2", h)], writes=[("A2", h)])
                    S.op("vector", lambda e: e.tensor_tensor(out=t1v, in0=XR, in1=ki, op=ALU.mult), reads=B(4, 5) + kk + ["t1"], writes=["t1"])
                    S.op("vector", lambda e: e.tensor_tensor(out=t2v, in0=XI, in1=kr, op=ALU.mult), reads=B(6, 7) + kk + ["t2"], writes=["t2"])
                    S.op("gpsimd", lambda e: e.tensor_tensor(out=zi, in0=t1v, in1=t2v, op=ALU.add), reads=["t1", "t2", ("A2", h)], writes=[("A2", h)])

                yield from fft_forward(vc, ["vc"], mulk)
                for h in range(2):
                    fns = []
                    for gi in range(8):
                        g = h * 8 + gi
                        o1 = bank(4, 4)[:, gi * 256:gi * 256 + 128]
                        o2 = bank(4, 4)[:, gi * 256 + 128:gi * 256 + 256]
                        fns.append(lambda e, g=g, o1=o1: e.matmul(out=o1, lhsT=Zt[:, 0, g, :], rhs=kC, start=True, stop=False))
                        fns.append(lambda e, g=g, o1=o1: e.matmul(out=o1, lhsT=Zt[:, 1, g, :], rhs=kSn, start=False, stop=True))
                        fns.append(lambda e, g=g, o2=o2: e.matmul(out=o2, lhsT=Zt[:, 0, g, :], rhs=kS, start=True, stop=False))
                        fns.append(lambda e, g=g, o2=o2: e.matmul(out=o2, lhsT=Zt[:, 1, g, :], rhs=kC, start=False, stop=True))
                    S.mm(fns, reads=[("A2", h), "kmats"], writes=B(4, 5, 6, 7))
                    for gi in range(8):
                        g = h * 8 + gi
                        for ri in range(2):
                            srcv = bank(4, 4)[:, gi * 256 + ri * 128: gi * 256 + ri * 128 + 128].rearrange("p (a c) -> p a c", c=8)
                            dstv = Bbig[:, ri, :, g * 8:(g + 1) * 8]
                            S.op("scalar", lambda e, dstv=dstv, srcv=srcv: e.activation(out=dstv, in_=srcv, func=AF.Copy),
                                 reads=B(4 + gi // 2), writes=[("A_sb", ri)])
                    yield
                fns = []
                for a in range(16):
                    o = bank(4, 4)[:, a * 128:(a + 1) * 128]
                    fns.append(lambda e, a=a, o=o: e.matmul(out=o, lhsT=gw[:, a, 0, :], rhs=Bbig[:, 0, a, :], start=True, stop=False))
                    fns.append(lambda e, a=a, o=o: e.matmul(out=o, lhsT=gw[:, a, 1, :], rhs=Bbig[:, 1, a, :], start=False, stop=True))
                S.mm(fns, reads=[("A_sb", 0), ("A_sb", 1), "gw"], writes=B(4, 5, 6, 7))
                S.op("scalar", lambda e: e.activation(out=ytok.rearrange("p a c -> p (a c)"), in_=bank(4, 4), func=AF.Copy),
                     reads=B(4, 5, 6, 7), writes=["vT"])
                yield
                fns = []
                for a in range(16):
                    fns.append(lambda e, a=a: e.transpose(out=bankb(4, 2)[:, a * 128:(a + 1) * 128], in_=ytok[:, a, :], identity=ident))
                S.mm(fns, reads=["vT", "kmats"], writes=B(4, 5))
                S.op("vector", lambda e, cc=cc, x0b=x0b: e.tensor_tensor(out=ymixT[:, cc, :].rearrange("p (q a) -> p q a", a=16),
                                                                in0=bankb(4, 2).rearrange("p (a q) -> p q a", q=128),
                                                                in1=x0b.rearrange("p (q a) -> p q a", a=16), op=ALU.mult),
                     reads=B(4, 5) + [x0k], writes=[("ymixT", cc)])
                if b == 0 and cc == 0:
                    dump("ym0", ymixT[:, 0, :], [("ymixT", 0)])
                yield

            ncc = 4 if stop not in ("hy1", "hyA", "hyB", "hyC", "hyD") else 1
            run_gen(hy_inproj(0))
            for cc in range(ncc):
                if cc + 1 < ncc:
                    interleave(hy_fft(cc), hy_inproj(cc + 1))
                else:
                    run_gen(hy_fft(cc))
            if stop in ("hyena", "hy1", "hyA", "hyB", "hyC", "hyD"):
                continue
            S.barrier()
            for j in range(4):
                bk = inproj(16 + j)
                ks = B(bk, bk + 1, bk + 2, bk + 3)
                S.op("scalar", lambda e, j=j, bk=bk: e.activation(out=qT[:, :, j * 128:(j + 1) * 128], in_=bank(bk, 4).rearrange("p (i q) -> p i q", q=128), func=AF.Copy), reads=ks, writes=["qT"])
            bk = inproj(20)
            S.op("vector", lambda e, bk=bk: e.tensor_copy(out=kT, in_=bank(bk, 4)), reads=B(bk, bk + 1, bk + 2, bk + 3), writes=["kT"])
            bk = inproj(21)
            S.op("scalar", lambda e, bk=bk: e.activation(out=vTa, in_=bank(bk, 4), func=AF.Copy), reads=B(bk, bk + 1, bk + 2, bk + 3), writes=["vTa"])
            fns = []
            for tt in range(16):
                fns.append(lambda e, tt=tt: e.transpose(out=bankb(3)[:, 0:128] if False else bankb(6, 2)[:, tt * 128:(tt + 1) * 128],
                                                        in_=vTa[:, tt * 128:(tt + 1) * 128], identity=ident))
            S.mm(fns, reads=["vTa", "kmats"], writes=B(6, 7))
            S.op("vector", lambda e: e.tensor_copy(out=vtok.rearrange("p a c -> p (a c)"), in_=bankb(6, 2)), reads=B(6, 7), writes=["vtok"])
            units = [(i, kv) for i in range(16) for kv in range(2)]

            def unit_info(u):
                i, kv = units[u]
                kbs = [kb for kb in (i - 1, i, i + 1) if 0 <= kb < 16]
                return i, kv, kbs, len(kbs), kbs[0] - (i - 1), slice(kv * 64, (kv + 1) * 64), (0 if u % 2 == 0 else 4), u % 2

            def unit_a(u):
                i, kv, kbs, nk, s0, rows, sb0, psl = unit_info(u)
                fns = []
                for si, kb in enumerate(kbs):
                    fns.append(lambda e, si=si, kb=kb, rows=rows, sb0=sb0, i=i: e.matmul(
                        out=bank(sb0 + si), lhsT=kT[rows, kb * 128:(kb + 1) * 128],
                        rhs=qT[rows, i, :], start=True, stop=True))
                sk = B(*[sb0 + si for si in range(nk)])
                S.mm(fns, reads=["kT", "qT"], writes=sk)
                S.op("scalar", lambda e, sb0=sb0, nk=nk, psl=psl: e.activation(out=Pt[:, psl, 0:nk, :].rearrange("p s n -> p (s n)"),
                                                                                in_=bank(sb0, nk), func=AF.Exp, scale=0.125),
                     reads=sk, writes=[("P", psl)])
                S.op("vector", lambda e, nk=nk, psl=psl, kv=kv, s0=s0: e.tensor_tensor(
                    out=Pt[:, psl, 0:nk, :].rearrange("p s (g q) -> p s g q", q=128), in0=Pt[:, psl, 0:nk, :].rearrange("p s (g q) -> p s g q", q=128),
                    in1=dtab[:, kv, s0:s0 + nk, :, :], op=ALU.mult), reads=[("P", psl), "dtab"], writes=[("P", psl)])

            def unit_b(u):
                i, kv, kbs, nk, s0, rows, sb0, psl = unit_info(u)
                fns = []
                for si, kb in enumerate(kbs):
                    fns.append(lambda e, si=si, kb=kb, psl=psl: e.matmul(out=bank(3), lhsT=vtok[:, kb, :], rhs=Pt[:, psl, si, :],
                                                                         start=(si == 0), stop=(si == nk - 1)))
                for si, kb in enumerate(kbs):
                    fns.append(lambda e, si=si, psl=psl: e.matmul(out=bank(7), lhsT=ones, rhs=Pt[:, psl, si, :], start=(si == 0), stop=False))
                fns.append(lambda e, kv=kv: e.matmul(out=bank(7), lhsT=ones, rhs=esinkT[:, kv * 4:(kv + 1) * 4, :].rearrange("p g q -> p (g q)"),
                                                     start=False, stop=True))
                S.mm(fns, reads=[("P", psl), "vtok", "kmats", "esinkT"], writes=B(3, 7))
                S.op("scalar", lambda e, rows=rows, psl=psl: e.activation(out=rden[rows, psl, :], in_=bank(7)[rows, :], func=AF.Ln),
                     reads=B(7), writes=[("rden", psl)])
                S.op("scalar", lambda e, rows=rows, psl=psl: e.activation(out=rden[rows, psl, :], in_=rden[rows, psl, :], func=AF.Exp, scale=-1.0),
                     reads=[("rden", psl)], writes=[("rden", psl)])
                S.op("vector", lambda e, rows=rows, psl=psl, i=i: e.tensor_tensor(
                    out=ymixT[rows, 4:8, i * 128:(i + 1) * 128], in0=bank(3)[rows, :].rearrange("p (g q) -> p g q", q=128),
                    in1=rden[rows, psl, :].rearrange("p (g q) -> p g q", q=128), op=ALU.mult),
                    reads=B(3) + [("rden", psl)], writes=[("ymixT", "a")])

            unit_a(0)
            for u in range(len(units)):
                if u + 1 < len(units):
                    unit_a(u + 1)
                unit_b(u)
            for j in range(4):
                bk = inproj(22 + j)
                silu_evac(bk, sg, "sg")
                S.op("gpsimd", lambda e, j=j: e.tensor_tensor(out=ymixT[:, 4 + j, :], in0=ymixT[:, 4 + j, :], in1=sg, op=ALU.mult),
                     reads=["sg", ("ymixT", "a")], writes=[("ymixT", "a")])
            if b == 0:
                dump("ymixT", ymixT[:, :, :], [("ymixT", c) for c in (0, 1, 2, 3, "a")])
            if stop == "attn":
                continue
            S.barrier()
            if b + 1 < nb:
                for tt in range(16):
                    phase5_tile(b, tt, 2 + tt % 2, 0 if tt % 2 == 0 else 4)
                    phase1_tile(b + 1, tt, tt % 2)
            else:
                for tt in range(16):
                    phase5_tile(b, tt, 2 + tt % 2, 0 if tt % 2 == 0 else 4)

        S.barrier()
        S.wait_all_dma("sync", [k for k in (("outst", 0), ("outst", 1), "dbg") if k in S.dsem])
        S.emit()
    return nc, dbg_out


def col_chunks():
    o_hg, o_q, o_k, o_v, o_ag = 1536, 2048, 2560, 2688, 2816
    ch = []
    for cc in range(4):
        for base in (0, 512, 1024, o_hg):
            ch.append(np.arange(base + cc * 128, base + (cc + 1) * 128))
    for j in range(4):
        ch.append(np.concatenate([o_q + j * 64 + np.arange(64), o_q + (j + 4) * 64 + np.arange(64)]))
    ch.append(o_k + np.arange(128))
    ch.append(o_v + np.arange(128))
    for j in range(4):
        ch.append(np.concatenate([o_ag + j * 64 + np.arange(64), o_ag + (j + 4) * 64 + np.arange(64)]))
    return ch


def mix_rows():
    rows = [np.arange(cc * 128, (cc + 1) * 128) for cc in range(4)]
    for j in range(4):
        rows.append(np.concatenate([512 + j * 64 + np.arange(64), 512 + (j + 4) * 64 + np.arange(64)]))
    return rows


def prep_shared(inp):
    c = consts()
    f = lambda a: np.ascontiguousarray(np.asarray(a, dtype=np.float32))
    w_in = f(inp["w_in"])[0]
    chs = col_chunks()
    win = np.stack([w_in[:, cols].reshape(8, 128, 128).transpose(1, 0, 2) for cols in chs], axis=0)
    w_out = f(inp["w_out"])[0]
    wout = np.stack([w_out[r, :] for r in mix_rows()], axis=1)
    w_short = f(inp["w_short"])[0]
    b_short = f(inp["b_short"])[0]
    convw = np.zeros((128, 12, 4), np.float32)
    for ty in range(3):
        for cc in range(4):
            idx = ty * 512 + cc * 128 + np.arange(128)
            convw[:, ty * 4 + cc, 0:3] = w_short[:, idx].T
            convw[:, ty * 4 + cc, 3] = b_short[idx]
    sink = f(inp["attn_sink"])[0]
    d = dict(
        win=np.ascontiguousarray(win), wout=np.ascontiguousarray(wout),
        preg=np.ascontiguousarray(f(inp["pre_g"])[0].reshape(8, 128).T),
        postg=np.ascontiguousarray(np.broadcast_to(f(inp["post_g"])[0][None, :], (128, DM))),
        convw=convw,
        hd=np.ascontiguousarray(f(inp["hyena_d"])[0].reshape(4, 128).T),
        sink=np.ascontiguousarray(np.broadcast_to(sink[None, :], (128, 8))),
        wf1=f(inp["w_f1"])[0], wf2=f(inp["w_f2"])[0], wf3=f(inp["w_f3"])[0], wf4=f(inp["w_f4"])[0],
        fb=np.ascontiguousarray(np.stack([f(inp["b_f1"])[0], f(inp["b_f2"])[0], f(inp["b_f3"])[0]], axis=1)),
        fr=np.ascontiguousarray(f(inp["sin_freq"])[0].T),
        zT=c["zT"], decay=c["decay"], fw=c["fw"], gw=c["gw"], kmats=c["kmats"], dtab=c["dtab"],
    )
    return d


def kernel(**inputs):
    x = np.asarray(inputs["x"], dtype=np.float32)
    shared = prep_shared(inputs)
    nc, _ = build()
    in_maps = []
    for c in range(NCORE):
        m = dict(shared)
        m["x"] = np.ascontiguousarray(x[c * NB:(c + 1) * NB])
        in_maps.append(m)
    res = run_bass_kernel_spmd(nc, in_maps, core_ids=list(range(NCORE)))
    return np.concatenate([r["out"] for r in res.results], axis=0).astype(np.float32)
```

```python
import math
from contextlib import ExitStack

import numpy as np
import ml_dtypes

import concourse.bass as bass
import concourse.mybir as mybir
from concourse.bass_utils import run_bass_kernel_spmd

F32 = mybir.dt.float32
BF16 = mybir.dt.bfloat16
I32 = mybir.dt.int32
AF = mybir.ActivationFunctionType
ALU = mybir.AluOpType
NPBF = ml_dtypes.bfloat16

NCORE = 8
NB = 4
L = 2048
DM = 1024
DH = 512
NCH = 26
NFFT = 4096
EPS = 1e-6

ENGS = ("tensor", "vector", "scalar", "gpsimd", "sync")


class Sched:
    def __init__(self, nc, stack):
        self.nc = nc
        self.stack = stack
        self.q = {e: [] for e in ENGS}
        self.esem = {e: stack.enter_context(nc.semaphore("s_" + e)) for e in ENGS}
        self.ecnt = {e: 0 for e in ENGS}
        self.seen = {e: {} for e in ENGS}
        self.dsem = {}
        self.lastw = {}
        self.readers = {}

    def _need(self, eng, ev, waits):
        if ev is None:
            return
        sem, val = ev
        if eng == "tensor" and sem is self.esem["tensor"]:
            return
        k = id(sem)
        if self.seen[eng].get(k, 0) >= val:
            return
        cur = waits.get(k)
        if cur is None or cur[1] < val:
            waits[k] = (sem, val)

    def _deps(self, eng, reads, writes):
        waits = {}
        for k in reads:
            self._need(eng, self.lastw.get(k), waits)
        for k in writes:
            self._need(eng, self.lastw.get(k), waits)
            for ev in self.readers.get(k, ()):
                self._need(eng, ev, waits)
        for k, (sem, val) in waits.items():
            self.seen[eng][k] = val
            self.q[eng].append(lambda e, sem=sem, val=val: e.wait_ge(sem, val))

    def _commit(self, ev, reads, writes):
        for k in reads:
            self.readers.setdefault(k, []).append(ev)
        for k in writes:
            self.lastw[k] = ev
            self.readers[k] = []

    def op(self, eng, fn, reads=(), writes=()):
        self._deps(eng, reads, writes)
        self.ecnt[eng] += 1
        sem, val = self.esem[eng], self.ecnt[eng]
        self.q[eng].append(lambda e, fn=fn, sem=sem: fn(e).then_inc(sem, 1))
        self._commit((sem, val), reads, writes)

    def mm(self, fns, reads=(), writes=()):
        eng = "tensor"
        self._deps(eng, reads, writes)
        for fn in fns[:-1]:
            self.q[eng].append(lambda e, fn=fn: fn(e))
        self.ecnt[eng] += 1
        sem, val = self.esem[eng], self.ecnt[eng]
        fn = fns[-1]
        self.q[eng].append(lambda e, fn=fn, sem=sem: fn(e).then_inc(sem, 1))
        self._commit((sem, val), reads, writes)

    def dma(self, eng, out, in_, semkey, reads=(), writes=()):
        self._deps(eng, reads, writes)
        if semkey not in self.dsem:
            self.dsem[semkey] = [self.stack.enter_context(self.nc.semaphore("d_%d" % len(self.dsem))), 0]
        ent = self.dsem[semkey]
        ent[1] += 16
        sem, val = ent[0], ent[1]
        self.q[eng].append(lambda e, out=out, in_=in_, sem=sem: e.dma_start(out=out, in_=in_).then_inc(sem, 16))
        self._commit((sem, val), reads, writes)

    def wait_all_dma(self, eng, keys):
        for k in keys:
            sem, val = self.dsem[k]
            if self.seen[eng].get(id(sem), 0) < val:
                self.seen[eng][id(sem)] = val
                self.q[eng].append(lambda e, sem=sem, val=val: e.wait_ge(sem, val))

    def barrier(self):
        for eng in ENGS:
            for o in ENGS:
                if self.ecnt[o] == 0 or (o == eng == "tensor"):
                    continue
                sem, val = self.esem[o], self.ecnt[o]
                if self.seen[eng].get(id(sem), 0) < val:
                    self.seen[eng][id(sem)] = val
                    self.q[eng].append(lambda e, sem=sem, val=val: e.wait_ge(sem, val))
            for k, (sem, val) in self.dsem.items():
                if val and self.seen[eng].get(id(sem), 0) < val:
                    self.seen[eng][id(sem)] = val
                    self.q[eng].append(lambda e, sem=sem, val=val: e.wait_ge(sem, val))

    def emit(self):
        with self.nc.Block() as block:
            for name in ENGS:
                q = self.q[name]
                if not q:
                    continue

                def body(e, q=q):
                    for th in q:
                        th(e)
                getattr(block, name)(body)


_CONST = {}


def consts():
    if _CONST:
        return _CONST
    n1 = np.arange(128, dtype=np.float64)
    n2 = np.arange(16, dtype=np.float64)
    k1 = np.arange(128, dtype=np.float64)
    th = 2 * np.pi * (16 * n1[:, None, None] + n2[None, :, None]) * (k1[None, None, :] + 0.5) / NFFT
    fw = np.stack([np.cos(th), -np.sin(th)], axis=2)
    gw = (2.0 / NFFT) * np.transpose(fw, (3, 1, 2, 0))
    ph = 2 * np.pi * np.outer(n2, n2) / 16
    eye8 = np.eye(8)
    kC = np.kron(np.cos(ph), eye8)
    kS = np.kron(np.sin(ph), eye8)
    kmats = np.stack([kC, kS, -kS, kC, np.eye(128), np.ones((128, 128))], axis=1)
    _CONST["fw"] = np.ascontiguousarray(fw).astype(NPBF)
    _CONST["gw"] = np.ascontiguousarray(gw).astype(NPBF)
    _CONST["kmats"] = np.ascontiguousarray(kmats).astype(NPBF)
    slopes = np.exp2(-8.0 * np.arange(1, 9, dtype=np.float64) / 8).reshape(2, 4)
    pk = np.arange(128)[:, None]
    pq = np.arange(128)[None, :]
    dt = np.zeros((128, 2, 3, 4, 128))
    for s in range(3):
        rel = np.abs((s - 1) * 128 + pk - pq)
        for kv in range(2):
            for g in range(4):
                dt[:, kv, s, g, :] = np.where(rel <= 128, np.exp(-slopes[kv, g] * rel), 0.0)
    _CONST["dtab"] = dt.astype(NPBF)
    t = np.arange(L, dtype=np.float32)
    tn = t / np.float32(L - 1)
    w = np.float32(2.0 * math.pi) * t / np.float32(L)
    bands = np.linspace(1e-4, 15, 16).astype(np.float32)
    ang = w[:, None] * bands[None, :]
    z = np.concatenate([tn[:, None], np.cos(ang), -np.sin(ang)], axis=-1).astype(np.float32)
    _CONST["zT"] = np.ascontiguousarray(z.T)
    min_decay = math.log(1e-2) / 1.5
    max_decay = math.log(1e-2) / 0.3
    deltas = np.abs(np.linspace(min_decay, max_decay, DH)).astype(np.float32)
    decay = np.exp(-tn[:, None] * deltas[None, :]).astype(np.float32)
    _CONST["decay"] = np.ascontiguousarray(decay.T.reshape(4, 128, L).transpose(1, 0, 2))
    return _CONST


def build(nb=NB, dbg=(), stop=None):
    nc = bass.Bass("TRN2", target_bir_lowering=False)

    def din(name, shape, dt=F32):
        return nc.dram_tensor(name, list(shape), dt, kind="ExternalInput").ap()

    x_d = din("x", [nb, L, DM])
    win_d = din("win", [NCH, 128, 8, 128])
    wout_d = din("wout", [128, 8, DM])
    preg_d = din("preg", [128, 8])
    postg_d = din("postg", [128, DM])
    convw_d = din("convw", [128, 12, 4])
    hd_d = din("hd", [128, 4])
    sink_d = din("sink", [128, 8])
    wf1_d = din("wf1", [33, 64])
    wf2_d = din("wf2", [64, 64])
    wf3_d = din("wf3", [64, 64])
    wf4_d = din("wf4", [64, 1024])
    fb_d = din("fb", [64, 3])
    fr_d = din("fr", [64, 3])
    zT_d = din("zT", [33, L])
    decay_d = din("decay", [128, 4, L])
    fw_d = din("fw", [128, 16, 2, 128], BF16)
    gw_d = din("gw", [128, 16, 2, 128], BF16)
    kmats_d = din("kmats", [128, 6, 128], BF16)
    dtab_d = din("dtab", [128, 2, 3, 4, 128], BF16)
    out_d = nc.dram_tensor("out", [nb, L, DM], F32, kind="ExternalOutput").ap()
    kfs_d = nc.dram_tensor("kfs", [4, 128, 16 * 2 * 128], BF16).ap()
    wsc_d = nc.dram_tensor("wsc", [NCH, 128, 8 * 128], BF16).ap()
    dbg_out = {}

    with ExitStack() as st:
        S = Sched(nc, st)

        def sb(name, shape, dt):
            return st.enter_context(nc.sbuf_tensor("sb_" + name, list(shape), dt))

        kmats = sb("kmats", [128, 6, 128], BF16)
        fw = sb("fw", [128, 16, 2, 128], BF16)
        gw = sb("gw", [128, 16, 2, 128], BF16)
        dtab = sb("dtab", [128, 2, 3, 4, 128], BF16)
        esinkT = sb("esinkT", [128, 8, 128], BF16)
        woutb = sb("woutb", [128, 8, DM], BF16)
        postg = sb("postg", [128, DM], F32)
        preg = sb("preg", [128, 8], F32)
        convw = sb("convw", [128, 12, 4], F32)
        hd = sb("hd", [128, 4], F32)
        sinkt = sb("sinkt", [128, 8], F32)
        small = sb("small", [128, 64], F32)
        kfslot = sb("kfslot", [128, 16, 2, 128], BF16)
        wst = sb("wst", [128, 2, 8, 128], F32)
        wbf = sb("wbf", [128, 2, 8, 128], BF16)
        xs = sb("xs", [128, 2, DM], F32)
        xn = sb("xn", [128, DM], BF16)
        junk = sb("junk", [128, DM], BF16)
        outt = sb("outt", [128, 2, DM], F32)
        ss = sb("ss", [128, 32], F32)
        rstd = sb("rstd", [128, 32], F32)
        hT = sb("hT", [128, 8, L], BF16)
        ymixT = sb("ymixT", [128, 8, L], BF16)
        REG_BF = 22 * 1024
        reg = sb("reg", [128, REG_BF], BF16)

        def carve_bf(off_kb, shape):
            n = int(np.prod(shape[1:]))
            o = off_kb * 512
            ap = reg[:, o:o + n]
            if len(shape) == 3:
                ap = ap.rearrange("p (a b) -> p a b", b=shape[2])
            elif len(shape) == 4:
                ap = ap.rearrange("p (a b c) -> p a b c", b=shape[2], c=shape[3])
            return ap

        def carve_f32(off_kb, shape):
            n = int(np.prod(shape[1:]))
            o = off_kb * 512
            ap = reg[:, o:o + 2 * n].bitcast(F32)
            if len(shape) == 3:
                ap = ap.rearrange("p (a b) -> p a b", b=shape[2])
            return ap

        x0c = carve_bf(0, [128, L])
        x1c = carve_bf(4, [128, L])
        vc = carve_bf(8, [128, L])
        sg = carve_bf(12, [128, L])
        vT = carve_bf(16, [128, 16, 128])
        A_sb = carve_bf(20, [128, 2, 16, 128])
        A2 = carve_bf(28, [128, 2, 16, 128])
        t1 = carve_bf(36, [128, 1024])
        t2 = carve_bf(38, [128, 1024])
        x0c2 = carve_bf(40, [128, L])
        x0bufs = [x0c, x0c2]
        ytok = vT
        Bbig = A_sb
        Zt = A2
        qT = carve_bf(0, [128, 16, 512])
        kT = carve_bf(16, [128, L])
        vTa = carve_bf(20, [128, L])
        vtok = carve_bf(24, [128, 16, 128])
        Pt = carve_bf(28, [128, 2, 3, 512])
        rden = carve_f32(34, [128, 2, 512])

        ps = st.enter_context(nc.psum_tensor("ps", [128, 4096], F32))
        psb = ps[:, :].bitcast(BF16)

        def bank(i, n=1):
            return ps[:, i * 512:(i + n) * 512]

        def bankb(i, n=1):
            return psb[:, i * 1024:(i + n) * 1024]

        def B(*idx):
            return [("B", i) for i in idx]

        def dump(name, ap, keys):
            if name not in dbg:
                return
            d = nc.dram_tensor("dbg_" + name, list(ap.shape), ap.dtype, kind="ExternalOutput").ap()
            dbg_out[name] = d
            S.dma("sync", d, ap, "dbg", reads=keys)

        ident = kmats[:, 4, :]
        ones = kmats[:, 5, :]
        kC = kmats[:, 0, :]
        kS = kmats[:, 1, :]
        kSn = kmats[:, 2, :]
        kCS = kmats[:, 0:2, :].rearrange("p a b -> p (a b)")
        kSnC = kmats[:, 2:4, :].rearrange("p a b -> p (a b)")

        def cload(t_ap, d_ap, key):
            S.dma("sync", t_ap, d_ap, "const", writes=[key])

        cload(kmats[:], kmats_d, "kmats")
        cload(fw[:], fw_d, "fw")
        cload(gw[:], gw_d, "gw")
        cload(dtab[:], dtab_d, "dtab")
        cload(postg[:], postg_d, "postg")
        cload(preg[:], preg_d, "preg")
        cload(convw[:], convw_d, "convw")
        cload(hd[:], hd_d, "hd")
        cload(sinkt[:], sink_d, "sinkt")

        for ch in range(8):
            sl = ch % 2
            S.dma("sync", xs[:, sl, :], wout_d[:, ch, :], ("xs", sl), writes=[("xs", sl)])
            S.op("scalar", lambda e, ch=ch, sl=sl: e.activation(out=woutb[:, ch, :], in_=xs[:, sl, :], func=AF.Copy, scale=0.5),
                 reads=[("xs", sl)], writes=["woutb"])

        S.op("scalar", lambda e: e.activation(out=small[:, 0:8], in_=sinkt[:], func=AF.Exp), reads=["sinkt"], writes=["small_es"])
        S.op("vector", lambda e: e.tensor_scalar_mul(out=small[:, 0:8], in0=small[:, 0:8], scalar1=1.0 / 128),
             reads=["small_es"], writes=["small_es"])
        S.op("vector", lambda e: e.tensor_copy(out=esinkT[:], in_=small[:, 0:8].unsqueeze(2).to_broadcast([128, 8, 128])),
             reads=["small_es"], writes=["esinkT"])

        def fft_forward(src, src_keys, on_half):
            fns = []
            for a in range(16):
                fns.append(lambda e, a=a: e.transpose(out=bankb(4, 2)[:, a * 128:(a + 1) * 128], in_=src[:, a:L:16], identity=ident))
            S.mm(fns, reads=list(src_keys) + ["kmats"], writes=B(4, 5))
            S.op("scalar", lambda e: e.activation(out=vT.rearrange("p a c -> p (a c)"), in_=bankb(4, 2), func=AF.Copy),
                 reads=B(4, 5), writes=["vT"])
            yield
            for ri in range(2):
                bk = 4
                fns = []
                for a in range(16):
                    fns.append(lambda e, a=a, ri=ri, bk=bk: e.matmul(out=bank(bk, 4)[:, a * 128:(a + 1) * 128], lhsT=fw[:, a, ri, :],
                                                                      rhs=vT[:, a, :], start=True, stop=True))
                S.mm(fns, reads=["vT", "fw"], writes=B(bk, bk + 1, bk + 2, bk + 3))
                srcv = bank(bk, 4).rearrange("p (a g c) -> p a g c", g=16, c=8)
                dstv = A_sb[:, ri, :, :].rearrange("p g (a c) -> p a g c", c=8)
                if ri == 0:
                    S.op("scalar", lambda e, srcv=srcv, dstv=dstv: e.activation(out=dstv, in_=srcv, func=AF.Copy),
                         reads=B(bk, bk + 1, bk + 2, bk + 3), writes=[("A_sb", ri)])
                else:
                    S.op("vector", lambda e, srcv=srcv, dstv=dstv: e.tensor_copy(out=dstv, in_=srcv),
                         reads=B(bk, bk + 1, bk + 2, bk + 3), writes=[("A_sb", ri)])
                yield
            fns = []
            for ri in range(2):
                for g in range(16):
                    o = (ri * 16 + g) * 128
                    fns.append(lambda e, g=g, ri=ri, o=o: e.transpose(out=bankb(4, 4)[:, o:o + 128],
                                                                       in_=A_sb[:, ri, g, :], identity=ident))
            S.mm(fns, reads=[("A_sb", 0), ("A_sb", 1), "kmats"], writes=B(4, 5, 6, 7))
            pv = bankb(4, 4).rearrange("p (r g k) -> p r g k", r=2, k=128)
            S.op("scalar", lambda e: e.activation(out=A2[:, :, 0:8, :], in_=pv[:, :, 0:8, :], func=AF.Copy),
                 reads=B(4, 5, 6, 7), writes=[("A2", 0)])
            S.op("vector", lambda e: e.tensor_copy(out=A2[:, :, 8:16, :], in_=pv[:, :, 8:16, :]),
                 reads=B(4, 5, 6, 7), writes=[("A2", 1)])
            yield
            for h in range(2):
                fns = []
                for q4 in range(2):
                    g0 = h * 8 + q4 * 4
                    ar = A2[:, 0, g0:g0 + 4, :].rearrange("p g k -> p (g k)")
                    ai = A2[:, 1, g0:g0 + 4, :].rearrange("p g k -> p (g k)")
                    xr = bank(4 + q4)
                    xi = bank(6 + q4)
                    fns.append(lambda e, ar=ar, xr=xr: e.matmul(out=xr, lhsT=kC, rhs=ar, start=True, stop=False))
                    fns.append(lambda e, ai=ai, xr=xr: e.matmul(out=xr, lhsT=kS, rhs=ai, start=False, stop=True))
                    fns.append(lambda e, ai=ai, xi=xi: e.matmul(out=xi, lhsT=kC, rhs=ai, start=True, stop=False))
                    fns.append(lambda e, ar=ar, xi=xi: e.matmul(out=xi, lhsT=kSn, rhs=ar, start=False, stop=True))
                S.mm(fns, reads=[("A2", h), "kmats"], writes=B(4, 5, 6, 7))
                on_half(h)
                yield

        def run_gen(g):
            for _ in g:
                pass

        def interleave(ga, gb):
            alive = [ga, gb]
            while alive:
                for g in list(alive):
                    try:
                        next(g)
                    except StopIteration:
                        alive.remove(g)

        XR = bank(4, 2).rearrange("p (g k) -> p g k", k=128)
        XI = bank(6, 2).rearrange("p (g k) -> p g k", k=128)

        hTf = hT[:, :, :].rearrange("p a b -> p (a b)")
        ymf = ymixT[:, :, :].rearrange("p a b -> p (a b)")

        def f32view(flat, off_kb, n):
            return flat[:, off_kb * 512: off_kb * 512 + 2 * n].bitcast(F32)

        zT = f32view(hTf, 0, L)
        hA = f32view(hTf, 8, L)
        hB = f32view(hTf, 16, L)
        tmpf = f32view(hTf, 24, L)
        kfr = f32view(ymf, 0, L)
        kbr = f32view(ymf, 8, L)
        dec = f32view(ymf, 16, L)
        wf4 = f32view(ymf, 24, 1024)
        wsm = f32view(ymf, 28, 256)
        fbr = small[:, 8:11]
        frr = small[:, 11:14]
        frb = small[:, 14:17]
        Hfw = outt[:, 0, :].bitcast(BF16).rearrange("p (g r k) -> p g r k", r=2, k=128)
        Hfw2 = outt[:, 1, :].bitcast(BF16).rearrange("p (g r k) -> p g r k", r=2, k=128)

        S.barrier()
        S.dma("sync", zT[0:33, :], zT_d, "const", writes=["zT"])
        S.dma("sync", wsm[0:33, 0:64], wf1_d, "const", writes=["wsm"])
        S.dma("sync", wsm[0:64, 64:128], wf2_d, "const", writes=["wsm"])
        S.dma("sync", wsm[0:64, 128:192], wf3_d, "const", writes=["wsm"])
        S.dma("sync", wf4[0:64, :], wf4_d, "const", writes=["wf4"])
        S.dma("sync", fbr[0:64, :], fb_d, "const", writes=["fbr"])
        S.dma("sync", frr[0:64, :], fr_d, "const", writes=["frr"])
        S.op("vector", lambda e: e.tensor_tensor(out=frb[0:64, :], in0=frr[0:64, :], in1=fbr[0:64, :], op=ALU.mult),
             reads=["fbr", "frr"], writes=["frb"])

        S.op("vector", lambda e: e.tensor_scalar_mul(out=frb[0:64, :], in0=frb[0:64, :], scalar1=1.0 / 9), reads=["frb"], writes=["frb"])
        S.op("vector", lambda e: e.tensor_scalar_mul(out=frr[0:64, :], in0=frr[0:64, :], scalar1=1.0 / 9), reads=["frr"], writes=["frr"])

        def sin_layer(li, lhsT, kdim, src, dst):
            for nchk in range(4):
                S.mm([lambda e, nchk=nchk: e.matmul(out=bank(nchk)[0:64, :], lhsT=lhsT, rhs=src[0:kdim, nchk * 512:(nchk + 1) * 512],
                                                     start=True, stop=True)],
                     reads=["wsm", ("h", li)], writes=B(nchk))
            S.op("scalar", lambda e: e.activation(out=dst[0:64, :], in_=bank(0, 4)[0:64, :], func=AF.Sin,
                                                   scale=frr[0:64, li:li + 1], bias=frb[0:64, li:li + 1]),
                 reads=B(0, 1, 2, 3) + ["frr", "frb"], writes=[("h", li + 1)])
            for _ in range(2):
                S.op("vector", lambda e: e.tensor_tensor(out=tmpf[0:64, :], in0=dst[0:64, :], in1=dst[0:64, :], op=ALU.mult),
                     reads=[("h", li + 1)], writes=["tmpf"])
                S.op("vector", lambda e: e.tensor_scalar(out=tmpf[0:64, :], in0=tmpf[0:64, :], scalar1=-4.0, scalar2=3.0,
                                                         op0=ALU.mult, op1=ALU.add), reads=["tmpf"], writes=["tmpf"])
                S.op("vector", lambda e: e.tensor_tensor(out=dst[0:64, :], in0=dst[0:64, :], in1=tmpf[0:64, :], op=ALU.mult),
                     reads=["tmpf", ("h", li + 1)], writes=[("h", li + 1)])

        S.lastw[("h", 0)] = S.lastw.get("zT")
        sin_layer(0, wsm[0:33, 0:64], 33, zT, hA)
        sin_layer(1, wsm[0:64, 64:128], 64, hA, hB)
        sin_layer(2, wsm[0:64, 128:192], 64, hB, hA)
        h3 = hA
        dump("h3", h3[0:64, :], [("h", 3)])

        def precast(chs):
            for ch in chs:
                sl = ch % 2
                S.dma("sync", wst[:, sl, :, :], win_d[ch], ("wst", sl), writes=[("wst", sl)])
                S.op("gpsimd", lambda e, sl=sl: e.tensor_tensor(out=wbf[:, sl, :, :], in0=wst[:, sl, :, :],
                                                                 in1=preg[:, :].unsqueeze(2).to_broadcast([128, 8, 128]), op=ALU.mult),
                     reads=[("wst", sl), "preg"], writes=[("wbf", sl)])
                S.dma("sync", wsc_d[ch], wbf[:, sl, :, :].rearrange("p a b -> p (a b)"), ("wsc_st", sl), reads=[("wbf", sl)], writes=[("wsc", ch)])

        for cc in range(4):
            precast(range(cc * 7, min(NCH, cc * 7 + 7)))
            S.dma("sync", dec[:, :], decay_d[:, cc, :], "dec", writes=["dec"])
            for fb_i, dstk in ((0, kfr), (1, kbr)):
                col = fb_i * 512 + cc * 128
                for nchk in range(4):
                    S.mm([lambda e, nchk=nchk, col=col: e.matmul(out=bank(nchk), lhsT=wf4[0:64, col:col + 128],
                                                                  rhs=h3[0:64, nchk * 512:(nchk + 1) * 512], start=True, stop=True)],
                         reads=["wf4", ("h", 3)], writes=B(nchk))
                S.op("vector", lambda e, dstk=dstk: e.tensor_tensor(out=dstk[:, :], in0=bank(0, 4), in1=dec[:, :], op=ALU.mult),
                     reads=B(0, 1, 2, 3) + ["dec"], writes=[("kraw", fb_i)])
            S.op("gpsimd", lambda e: e.memset(kbr[:, 0:1], 0.0), reads=[], writes=[("kraw", 1)])
            S.op("scalar", lambda e: e.activation(out=tmpf[:, :], in_=kfr[:, :], func=AF.Square), reads=[("kraw", 0)], writes=["tmpf"])
            S.op("vector", lambda e: e.reduce_sum(out=small[:, 20:21], in_=tmpf[:, :], axis=mybir.AxisListType.X), reads=["tmpf"], writes=["ssq0"])
            S.op("scalar", lambda e: e.activation(out=tmpf[:, :], in_=kbr[:, :], func=AF.Square), reads=[("kraw", 1)], writes=["tmpf"])
            S.op("vector", lambda e: e.reduce_sum(out=small[:, 21:22], in_=tmpf[:, :], axis=mybir.AxisListType.X), reads=["tmpf"], writes=["ssq1"])
            S.op("vector", lambda e: e.scalar_tensor_tensor(out=small[:, 22:23], in0=small[:, 20:21], scalar=1e-12, in1=small[:, 21:22],
                                                            op0=ALU.add, op1=ALU.add), reads=["ssq0", "ssq1"], writes=["ssq"])
            S.op("scalar", lambda e: e.activation(out=small[:, 22:23], in_=small[:, 22:23], func=AF.Ln), reads=["ssq"], writes=["ssq"])
            S.op("scalar", lambda e: e.activation(out=small[:, 23:24], in_=small[:, 22:23], func=AF.Exp, scale=-0.5), reads=["ssq"], writes=["krs"])
            S.op("vector", lambda e: e.tensor_scalar(out=x0c, in0=kfr[:, :], scalar1=small[:, 23:24], scalar2=None, op0=ALU.mult),
                 reads=[("kraw", 0), "krs"], writes=["x0c"])
            S.op("vector", lambda e, cc=cc: e.scalar_tensor_tensor(out=x0c[:, 0:1], in0=kfr[:, 0:1], scalar=small[:, 23:24], in1=hd[:, cc:cc + 1],
                                                                    op0=ALU.mult, op1=ALU.add), reads=[("kraw", 0), "krs", "hd", "x0c"], writes=["x0c"])
            S.op("vector", lambda e: e.tensor_scalar(out=x1c, in0=kbr[:, :], scalar1=small[:, 23:24], scalar2=None, op0=ALU.mult),
                 reads=[("kraw", 1), "krs"], writes=["x1c"])
            if cc == 0:
                dump("kf0", x0c, ["x0c"])
                dump("kb0", x1c, ["x1c"])

            def keep_f(h):
                dst = Hfw if h == 0 else Hfw2
                S.op("scalar", lambda e, dst=dst: e.activation(out=dst[:, :, 0, :], in_=XR, func=AF.Copy), reads=B(4, 5), writes=[("Hf", h)])
                S.op("vector", lambda e, dst=dst: e.tensor_copy(out=dst[:, :, 1, :], in_=XI), reads=B(6, 7), writes=[("Hf", h)])

            def keep_b(h):
                dst = Hfw if h == 0 else Hfw2
                S.op("vector", lambda e, dst=dst, h=h: e.tensor_tensor(out=kfslot[:, h * 8:(h + 1) * 8, 0, :], in0=XR, in1=dst[:, :, 0, :], op=ALU.add),
                     reads=B(4, 5) + [("Hf", h)], writes=[("kfslot", h)])
                S.op("vector", lambda e, dst=dst, h=h: e.tensor_tensor(out=kfslot[:, h * 8:(h + 1) * 8, 1, :], in0=dst[:, :, 1, :], in1=XI, op=ALU.subtract),
                     reads=B(6, 7) + [("Hf", h)], writes=[("kfslot", h)])

            run_gen(fft_forward(x0c, ["x0c"], keep_f))
            run_gen(fft_forward(x1c, ["x1c"], keep_b))
            S.dma("sync", kfs_d[cc], kfslot[:, :, :, :].rearrange("p g r k -> p (g r k)"), ("kfs", cc),
                  reads=[("kfslot", 0), ("kfslot", 1)], writes=[("kfs", cc)])
            if cc == 0:
                dump("Kf0", kfslot[:, :, :, :], [("kfslot", 0), ("kfslot", 1)])

        S.barrier()

        wcount = [0]

        def load_w(ch):
            sl = wcount[0] % 2
            wcount[0] += 1
            S.dma("sync", wbf[:, sl, :, :].rearrange("p a b -> p (a b)"), wsc_d[ch], ("wbf", sl), reads=[("wsc", ch)], writes=[("wbf", sl)])
            return sl

        ucount = [0]

        def inproj_gen(ch, bk):
            sl = load_w(ch)
            for tcn in range(4):
                fns = []
                for kc in range(8):
                    fns.append(lambda e, kc=kc, tcn=tcn, sl=sl, bk=bk: e.matmul(out=bank(bk + tcn), lhsT=wbf[:, sl, kc, :],
                                                                                  rhs=hT[:, kc, tcn * 512:(tcn + 1) * 512],
                                                                                  start=(kc == 0), stop=(kc == 7)))
                S.mm(fns, reads=[("wbf", sl), "hT"], writes=B(bk + tcn))
                yield

        def inproj(ch):
            bk = 4 * (ucount[0] % 2)
            ucount[0] += 1
            run_gen(inproj_gen(ch, bk))
            return bk

        def conv_evac(bk, dst, dkey, ci):
            U = bank(bk, 4)
            ks = B(bk, bk + 1, bk + 2, bk + 3)
            S.op("scalar", lambda e: e.activation(out=dst, in_=U, func=AF.Identity, scale=convw[:, ci, 1:2], bias=convw[:, ci, 3:4]),
                 reads=ks + ["convw"], writes=[dkey])
            S.op("vector", lambda e: e.scalar_tensor_tensor(out=dst[:, 1:L], in0=U[:, 0:L - 1], scalar=convw[:, ci, 0:1], in1=dst[:, 1:L],
                                                            op0=ALU.mult, op1=ALU.add), reads=ks + ["convw", dkey], writes=[dkey])
            S.op("vector", lambda e: e.scalar_tensor_tensor(out=dst[:, 0:L - 1], in0=U[:, 1:L], scalar=convw[:, ci, 2:3], in1=dst[:, 0:L - 1],
                                                            op0=ALU.mult, op1=ALU.add), reads=ks + ["convw", dkey], writes=[dkey])

        def silu_evac(bk, dst, dkey):
            U = bank(bk, 4)
            ks = B(bk, bk + 1, bk + 2, bk + 3)
            S.op("scalar", lambda e: e.activation(out=dst, in_=U, func=AF.Tanh, scale=0.5), reads=ks, writes=[dkey])
            S.op("vector", lambda e: e.scalar_tensor_tensor(out=dst, in0=dst, scalar=1.0, in1=U, op0=ALU.add, op1=ALU.mult),
                 reads=ks + [dkey], writes=[dkey])

        for b in range(nb if stop != "setup" else 0):
            for tt in range(16):
                sl = tt % 2
                S.dma("sync", xs[:, sl, :], x_d[b, tt * 128:(tt + 1) * 128, :], ("xs", sl), writes=[("xs", sl)])
                S.op("scalar", lambda e, sl=sl, tt=tt: e.activation(out=junk[:, :], in_=xs[:, sl, :], func=AF.Square, scale=1.0 / 32),
                     reads=[("xs", sl)], writes=["junk"])
                S.op("vector", lambda e, tt=tt: e.reduce_sum(out=ss[:, tt:tt + 1], in_=junk[:, :], axis=mybir.AxisListType.X),
                     reads=["junk"], writes=[("ss", tt)])
                S.op("vector", lambda e, tt=tt: e.tensor_scalar_add(out=ss[:, tt:tt + 1], in0=ss[:, tt:tt + 1], scalar1=EPS),
                     reads=[("ss", tt)], writes=[("ss", tt)])
                S.op("scalar", lambda e, tt=tt: e.activation(out=rstd[:, tt:tt + 1], in_=ss[:, tt:tt + 1], func=AF.Ln),
                     reads=[("ss", tt)], writes=[("rstd", tt)])
                S.op("scalar", lambda e, tt=tt: e.activation(out=rstd[:, tt:tt + 1], in_=rstd[:, tt:tt + 1], func=AF.Exp, scale=-0.5),
                     reads=[("rstd", tt)], writes=[("rstd", tt)])
                S.op("vector", lambda e, sl=sl, tt=tt: e.tensor_scalar(out=xn[:, :], in0=xs[:, sl, :], scalar1=rstd[:, tt:tt + 1], scalar2=None,
                                                                       op0=ALU.mult), reads=[("xs", sl), ("rstd", tt)], writes=["xn"])
                fns = []
                for kc in range(8):
                    fns.append(lambda e, kc=kc: e.transpose(out=bankb(3)[:, kc * 128:(kc + 1) * 128], in_=xn[:, kc * 128:(kc + 1) * 128], identity=ident))
                S.mm(fns, reads=["xn", "kmats"], writes=B(3))
                S.op("vector", lambda e, tt=tt: e.tensor_copy(out=hT[:, :, tt * 128:(tt + 1) * 128],
                                                              in_=bankb(3).rearrange("p (a b) -> p a b", b=128)), reads=B(3), writes=["hT"])
            if b == 0:
                dump("hT", hT[:, :, :], ["hT"])

            if stop == "phase1":
                continue
            def hy_inproj(cc):
                base = cc * 4
                x0b = x0bufs[cc % 2]
                x0k = ("x0c", cc % 2)
                yield from inproj_gen(base + 1, 0)
                conv_evac(0, x1c, "x1c", 1 * 4 + cc)
                yield from inproj_gen(base + 2, 0)
                conv_evac(0, vc, "vc", 2 * 4 + cc)
                S.op("gpsimd", lambda e: e.tensor_tensor(out=vc, in0=vc, in1=x1c, op=ALU.mult), reads=["vc", "x1c"], writes=["vc"])
                yield from inproj_gen(base + 3, 0)
                silu_evac(0, sg, "sg")
                yield from inproj_gen(base + 0, 0)
                conv_evac(0, x0b, x0k, 0 * 4 + cc)
                S.op("gpsimd", lambda e, x0b=x0b: e.tensor_tensor(out=x0b, in0=x0b, in1=sg, op=ALU.mult), reads=[x0k, "sg"], writes=[x0k])

            def hy_fft(cc):
                x0b = x0bufs[cc % 2]
                x0k = ("x0c", cc % 2)
                S.dma("sync", kfslot[:, :, :, :].rearrange("p g r k -> p (g r k)"), kfs_d[cc], "kfslot_ld",
                      reads=[("kfs", cc)], writes=[("kfslot", 0), ("kfslot", 1)])

                def mulk(h):
                    kr = kfslot[:, h * 8:(h + 1) * 8, 0, :]
                    ki = kfslot[:, h * 8:(h + 1) * 8, 1, :]
                    t1v = t1.rearrange("p (g k) -> p g k", k=128)
                    t2v = t2.rearrange("p (g k) -> p g k", k=128)
                    zr = Zt[:, 0, h * 8:(h + 1) * 8, :]
                    zi = Zt[:, 1, h * 8:(h + 1) * 8, :]
                    kk = [("kfslot", h)]
                    S.op("vector", lambda e: e.tensor_tensor(out=t1v, in0=XR, in1=kr, op=ALU.mult), reads=B(4, 5) + kk, writes=["t1"])
                    S.op("vector", lambda e: e.tensor_tensor(out=t2v, in0=XI, in1=ki, op=ALU.mult), reads=B(6, 7) + kk, writes=["t2"])
                    S.op("gpsimd", lambda e: e.tensor_tensor(out=zr, in0=t1v, in1=t2v, op=ALU.subtract), reads=["t1", "t2", ("A2", h)], writes=[("A2", h)])
                    S.op("vector", lambda e: e.tensor_tensor(out=t1v, in0=XR, in1=ki, op=ALU.mult), reads=B(4, 5) + kk + ["t1"], writes=["t1"])
                    S.op("vector", lambda e: e.tensor_tensor(out=t2v, in0=XI, in1=kr, op=ALU.mult), reads=B(6, 7) + kk + ["t2"], writes=["t2"])
                    S.op("gpsimd", lambda e: e.tensor_tensor(out=zi, in0=t1v, in1=t2v, op=ALU.add), reads=["t1", "t2", ("A2", h)], writes=[("A2", h)])

                yield from fft_forward(vc, ["vc"], mulk)
                for h in range(2):
                    fns = []
                    for gi in range(8):
                        g = h * 8 + gi
                        o1 = bank(4, 4)[:, gi * 256:gi * 256 + 128]
                        o2 = bank(4, 4)[:, gi * 256 + 128:gi * 256 + 256]
                        fns.append(lambda e, g=g, o1=o1: e.matmul(out=o1, lhsT=Zt[:, 0, g, :], rhs=kC, start=True, stop=False))
                        fns.append(lambda e, g=g, o1=o1: e.matmul(out=o1, lhsT=Zt[:, 1, g, :], rhs=kSn, start=False, stop=True))
                        fns.append(lambda e, g=g, o2=o2: e.matmul(out=o2, lhsT=Zt[:, 0, g, :], rhs=kS, start=True, stop=False))
                        fns.append(lambda e, g=g, o2=o2: e.matmul(out=o2, lhsT=Zt[:, 1, g, :], rhs=kC, start=False, stop=True))
                    S.mm(fns, reads=[("A2", h), "kmats"], writes=B(4, 5, 6, 7))
                    for gi in range(8):
                        g = h * 8 + gi
                        for ri in range(2):
                            srcv = bank(4, 4)[:, gi * 256 + ri * 128: gi * 256 + ri * 128 + 128].rearrange("p (a c) -> p a c", c=8)
                            dstv = Bbig[:, ri, :, g * 8:(g + 1) * 8]
                            S.op("scalar", lambda e, dstv=dstv, srcv=srcv: e.activation(out=dstv, in_=srcv, func=AF.Copy),
                                 reads=B(4 + gi // 2), writes=[("A_sb", ri)])
                    yield
                fns = []
                for a in range(16):
                    o = bank(4, 4)[:, a * 128:(a + 1) * 128]
                    fns.append(lambda e, a=a, o=o: e.matmul(out=o, lhsT=gw[:, a, 0, :], rhs=Bbig[:, 0, a, :], start=True, stop=False))
                    fns.append(lambda e, a=a, o=o: e.matmul(out=o, lhsT=gw[:, a, 1, :], rhs=Bbig[:, 1, a, :], start=False, stop=True))
                S.mm(fns, reads=[("A_sb", 0), ("A_sb", 1), "gw"], writes=B(4, 5, 6, 7))
                S.op("scalar", lambda e: e.activation(out=ytok.rearrange("p a c -> p (a c)"), in_=bank(4, 4), func=AF.Copy),
                     reads=B(4, 5, 6, 7), writes=["vT"])
                yield
                fns = []
                for a in range(16):
                    fns.append(lambda e, a=a: e.transpose(out=bankb(4, 2)[:, a * 128:(a + 1) * 128], in_=ytok[:, a, :], identity=ident))
                S.mm(fns, reads=["vT", "kmats"], writes=B(4, 5))
                S.op("vector", lambda e, cc=cc, x0b=x0b: e.tensor_tensor(out=ymixT[:, cc, :].rearrange("p (q a) -> p q a", a=16),
                                                                in0=bankb(4, 2).rearrange("p (a q) -> p q a", q=128),
                                                                in1=x0b.rearrange("p (q a) -> p q a", a=16), op=ALU.mult),
                     reads=B(4, 5) + [x0k], writes=[("ymixT", cc)])
                if b == 0 and cc == 0:
                    dump("ym0", ymixT[:, 0, :], [("ymixT", 0)])
                yield

            ncc = 4 if stop not in ("hy1", "hyA", "hyB", "hyC", "hyD") else 1
            run_gen(hy_inproj(0))
            for cc in range(ncc):
                if cc + 1 < ncc:
                    interleave(hy_fft(cc), hy_inproj(cc + 1))
                else:
                    run_gen(hy_fft(cc))
            if stop in ("hyena", "hy1", "hyA", "hyB", "hyC", "hyD"):
                continue
            S.barrier()
            for j in range(4):
                bk = inproj(16 + j)
                ks = B(bk, bk + 1, bk + 2, bk + 3)
                S.op("scalar", lambda e, j=j, bk=bk: e.activation(out=qT[:, :, j * 128:(j + 1) * 128], in_=bank(bk, 4).rearrange("p (i q) -> p i q", q=128), func=AF.Copy), reads=ks, writes=["qT"])
            bk = inproj(20)
            S.op("vector", lambda e, bk=bk: e.tensor_copy(out=kT, in_=bank(bk, 4)), reads=B(bk, bk + 1, bk + 2, bk + 3), writes=["kT"])
            bk = inproj(21)
            S.op("scalar", lambda e, bk=bk: e.activation(out=vTa, in_=bank(bk, 4), func=AF.Copy), reads=B(bk, bk + 1, bk + 2, bk + 3), writes=["vTa"])
            fns = []
            for tt in range(16):
                fns.append(lambda e, tt=tt: e.transpose(out=bankb(3)[:, 0:128] if False else bankb(6, 2)[:, tt * 128:(tt + 1) * 128],
                                                        in_=vTa[:, tt * 128:(tt + 1) * 128], identity=ident))
            S.mm(fns, reads=["vTa", "kmats"], writes=B(6, 7))
            S.op("vector", lambda e: e.tensor_copy(out=vtok.rearrange("p a c -> p (a c)"), in_=bankb(6, 2)), reads=B(6, 7), writes=["vtok"])
            units = [(i, kv) for i in range(16) for kv in range(2)]

            def unit_info(u):
                i, kv = units[u]
                kbs = [kb for kb in (i - 1, i, i + 1) if 0 <= kb < 16]
                return i, kv, kbs, len(kbs), kbs[0] - (i - 1), slice(kv * 64, (kv + 1) * 64), (0 if u % 2 == 0 else 4), u % 2

            def unit_a(u):
                i, kv, kbs, nk, s0, rows, sb0, psl = unit_info(u)
                fns = []
                for si, kb in enumerate(kbs):
                    fns.append(lambda e, si=si, kb=kb, rows=rows, sb0=sb0, i=i: e.matmul(
                        out=bank(sb0 + si), lhsT=kT[rows, kb * 128:(kb + 1) * 128],
                        rhs=qT[rows, i, :], start=True, stop=True))
                sk = B(*[sb0 + si for si in range(nk)])
                S.mm(fns, reads=["kT", "qT"], writes=sk)
                S.op("scalar", lambda e, sb0=sb0, nk=nk, psl=psl: e.activation(out=Pt[:, psl, 0:nk, :].rearrange("p s n -> p (s n)"),
                                                                                in_=bank(sb0, nk), func=AF.Exp, scale=0.125),
                     reads=sk, writes=[("P", psl)])
                S.op("vector", lambda e, nk=nk, psl=psl, kv=kv, s0=s0: e.tensor_tensor(
                    out=Pt[:, psl, 0:nk, :].rearrange("p s (g q) -> p s g q", q=128), in0=Pt[:, psl, 0:nk, :].rearrange("p s (g q) -> p s g q", q=128),
                    in1=dtab[:, kv, s0:s0 + nk, :, :], op=ALU.mult), reads=[("P", psl), "dtab"], writes=[("P", psl)])

            def unit_b(u):
                i, kv, kbs, nk, s0, rows, sb0, psl = unit_info(u)
                fns = []
                for si, kb in enumerate(kbs):
                    fns.append(lambda e, si=si, kb=kb, psl=psl: e.matmul(out=bank(3), lhsT=vtok[:, kb, :], rhs=Pt[:, psl, si, :],
                                                                         start=(si == 0), stop=(si == nk - 1)))
                for si, kb in enumerate(kbs):
                    fns.append(lambda e, si=si, psl=psl: e.matmul(out=bank(7), lhsT=ones, rhs=Pt[:, psl, si, :], start=(si == 0), stop=False))
                fns.append(lambda e, kv=kv: e.matmul(out=bank(7), lhsT=ones, rhs=esinkT[:, kv * 4:(kv + 1) * 4, :].rearrange("p g q -> p (g q)"),
                                                     start=False, stop=True))
                S.mm(fns, reads=[("P", psl), "vtok", "kmats", "esinkT"], writes=B(3, 7))
                S.op("scalar", lambda e, rows=rows, psl=psl: e.activation(out=rden[rows, psl, :], in_=bank(7)[rows, :], func=AF.Ln),
                     reads=B(7), writes=[("rden", psl)])
                S.op("scalar", lambda e, rows=rows, psl=psl: e.activation(out=rden[rows, psl, :], in_=rden[rows, psl, :], func=AF.Exp, scale=-1.0),
                     reads=[("rden", psl)], writes=[("rden", psl)])
                S.op("vector", lambda e, rows=rows, psl=psl, i=i: e.tensor_tensor(
                    out=ymixT[rows, 4:8, i * 128:(i + 1) * 128], in0=bank(3)[rows, :].rearrange("p (g q) -> p g q", q=128),
                    in1=rden[rows, psl, :].rearrange("p (g q) -> p g q", q=128), op=ALU.mult),
                    reads=B(3) + [("rden", psl)], writes=[("ymixT", "a")])

            unit_a(0)
            for u in range(len(units)):
                if u + 1 < len(units):
                    unit_a(u + 1)
                unit_b(u)
            for j in range(4):
                bk = inproj(22 + j)
                silu_evac(bk, sg, "sg")
                S.op("gpsimd", lambda e, j=j: e.tensor_tensor(out=ymixT[:, 4 + j, :], in0=ymixT[:, 4 + j, :], in1=sg, op=ALU.mult),
                     reads=["sg", ("ymixT", "a")], writes=[("ymixT", "a")])
            if b == 0:
                dump("ymixT", ymixT[:, :, :], [("ymixT", c) for c in (0, 1, 2, 3, "a")])
            if stop == "attn":
                continue
            S.barrier()
            ymk = [("ymixT", c) for c in (0, 1, 2, 3, "a")]
            for tt in range(16):
                sl = tt % 2
                S.dma("sync", xs[:, sl, :], x_d[b, tt * 128:(tt + 1) * 128, :], ("xs", sl), writes=[("xs", sl)])
                yb = 0 if tt % 2 == 0 else 4
                for hf in range(2):
                    fns = []
                    for ch in range(8):
                        fns.append(lambda e, ch=ch, hf=hf, tt=tt, yb=yb: e.matmul(out=bank(yb + hf), lhsT=ymixT[:, ch, tt * 128:(tt + 1) * 128],
                                                                                    rhs=woutb[:, ch, hf * 512:(hf + 1) * 512],
                                                                                    start=(ch == 0), stop=(ch == 7)))
                    S.mm(fns, reads=ymk + ["woutb"], writes=B(yb + hf))
                yk = B(yb, yb + 1)
                c0 = 16 + tt
                S.op("scalar", lambda e, yb=yb, c0=c0: e.activation(out=junk[:, :], in_=bank(yb, 2), func=AF.Square, scale=1.0 / 32),
                     reads=yk, writes=["junk"])
                S.op("vector", lambda e, c0=c0: e.reduce_sum(out=ss[:, c0:c0 + 1], in_=junk[:, :], axis=mybir.AxisListType.X),
                     reads=["junk"], writes=[("ss", c0)])
                S.op("vector", lambda e, c0=c0: e.tensor_scalar_add(out=ss[:, c0:c0 + 1], in0=ss[:, c0:c0 + 1], scalar1=EPS),
                     reads=[("ss", c0)], writes=[("ss", c0)])
                S.op("scalar", lambda e, c0=c0: e.activation(out=rstd[:, c0:c0 + 1], in_=ss[:, c0:c0 + 1], func=AF.Ln),
                     reads=[("ss", c0)], writes=[("rstd", c0)])
                S.op("scalar", lambda e, c0=c0: e.activation(out=rstd[:, c0:c0 + 1], in_=rstd[:, c0:c0 + 1], func=AF.Exp, scale=-0.5),
                     reads=[("rstd", c0)], writes=[("rstd", c0)])
                S.op("vector", lambda e, yb=yb, c0=c0, sl=sl: e.scalar_tensor_tensor(out=outt[:, sl, :], in0=bank(yb, 2), scalar=rstd[:, c0:c0 + 1],
                                                                                     in1=postg[:, :], op0=ALU.mult, op1=ALU.mult),
                     reads=yk + [("rstd", c0), "postg"], writes=[("outt", sl)])
                S.op("gpsimd", lambda e, sl=sl: e.tensor_tensor(out=outt[:, sl, :], in0=outt[:, sl, :], in1=xs[:, sl, :], op=ALU.add),
                     reads=[("outt", sl), ("xs", sl)], writes=[("outt", sl)])
                S.dma("sync", out_d[b, tt * 128:(tt + 1) * 128, :], outt[:, sl, :], ("outst", sl), reads=[("outt", sl)])

        S.barrier()
        S.wait_all_dma("sync", [k for k in (("outst", 0), ("outst", 1), "dbg") if k in S.dsem])
        S.emit()
    return nc, dbg_out


def col_chunks():
    o_hg, o_q, o_k, o_v, o_ag = 1536, 2048, 2560, 2688, 2816
    ch = []
    for cc in range(4):
        for base in (0, 512, 1024, o_hg):
            ch.append(np.arange(base + cc * 128, base + (cc + 1) * 128))
    for j in range(4):
        ch.append(np.concatenate([o_q + j * 64 + np.arange(64), o_q + (j + 4) * 64 + np.arange(64)]))
    ch.append(o_k + np.arange(128))
    ch.append(o_v + np.arange(128))
    for j in range(4):
        ch.append(np.concatenate([o_ag + j * 64 + np.arange(64), o_ag + (j + 4) * 64 + np.arange(64)]))
    return ch


def mix_rows():
    rows = [np.arange(cc * 128, (cc + 1) * 128) for cc in range(4)]
    for j in range(4):
        rows.append(np.concatenate([512 + j * 64 + np.arange(64), 512 + (j + 4) * 64 + np.arange(64)]))
    return rows


def prep_shared(inp):
    c = consts()
    f = lambda a: np.ascontiguousarray(np.asarray(a, dtype=np.float32))
    w_in = f(inp["w_in"])[0]
    chs = col_chunks()
    win = np.stack([w_in[:, cols].reshape(8, 128, 128).transpose(1, 0, 2) for cols in chs], axis=0)
    w_out = f(inp["w_out"])[0]
    wout = np.stack([w_out[r, :] for r in mix_rows()], axis=1)
    w_short = f(inp["w_short"])[0]
    b_short = f(inp["b_short"])[0]
    convw = np.zeros((128, 12, 4), np.float32)
    for ty in range(3):
        for cc in range(4):
            idx = ty * 512 + cc * 128 + np.arange(128)
            convw[:, ty * 4 + cc, 0:3] = w_short[:, idx].T
            convw[:, ty * 4 + cc, 3] = b_short[idx]
    sink = f(inp["attn_sink"])[0]
    d = dict(
        win=np.ascontiguousarray(win), wout=np.ascontiguousarray(wout),
        preg=np.ascontiguousarray(f(inp["pre_g"])[0].reshape(8, 128).T),
        postg=np.ascontiguousarray(np.broadcast_to(f(inp["post_g"])[0][None, :], (128, DM))),
        convw=convw,
        hd=np.ascontiguousarray(f(inp["hyena_d"])[0].reshape(4, 128).T),
        sink=np.ascontiguousarray(np.broadcast_to(sink[None, :], (128, 8))),
        wf1=f(inp["w_f1"])[0], wf2=f(inp["w_f2"])[0], wf3=f(inp["w_f3"])[0], wf4=f(inp["w_f4"])[0],
        fb=np.ascontiguousarray(np.stack([f(inp["b_f1"])[0], f(inp["b_f2"])[0], f(inp["b_f3"])[0]], axis=1)),
        fr=np.ascontiguousarray(f(inp["sin_freq"])[0].T),
        zT=c["zT"], decay=c["decay"], fw=c["fw"], gw=c["gw"], kmats=c["kmats"], dtab=c["dtab"],
    )
    return d


def kernel(**inputs):
    x = np.asarray(inputs["x"], dtype=np.float32)
    shared = prep_shared(inputs)
    nc, _ = build()
    in_maps = []
    for c in range(NCORE):
        m = dict(shared)
        m["x"] = np.ascontiguousarray(x[c * NB:(c + 1) * NB])
        in_maps.append(m)
    res = run_bass_kernel_spmd(nc, in_maps, core_ids=list(range(NCORE)))
    return np.concatenate([r["out"] for r in res.results], axis=0).astype(np.float32)
```

```python
import math
from contextlib import ExitStack

import numpy as np
import ml_dtypes

import concourse.bass as bass
import concourse.mybir as mybir
from concourse.bass_utils import run_bass_kernel_spmd

F32 = mybir.dt.float32
BF16 = mybir.dt.bfloat16
I32 = mybir.dt.int32
AF = mybir.ActivationFunctionType
ALU = mybir.AluOpType
NPBF = ml_dtypes.bfloat16

NCORE = 8
NB = 4
L = 2048
DM = 1024
DH = 512
NCH = 26
NFFT = 4096
EPS = 1e-6

ENGS = ("tensor", "vector", "scalar", "gpsimd", "sync")


class Sched:
    def __init__(self, nc, stack):
        self.nc = nc
        self.stack = stack
        self.q = {e: [] for e in ENGS}
        self.esem = {e: stack.enter_context(nc.semaphore("s_" + e)) for e in ENGS}
        self.ecnt = {e: 0 for e in ENGS}
        self.seen = {e: {} for e in ENGS}
        self.dsem = {}
        self.lastw = {}
        self.readers = {}

    def _need(self, eng, ev, waits):
        if ev is None:
            return
        sem, val = ev
        if eng == "tensor" and sem is self.esem["tensor"]:
            return
        k = id(sem)
        if self.seen[eng].get(k, 0) >= val:
            return
        cur = waits.get(k)
        if cur is None or cur[1] < val:
            waits[k] = (sem, val)

    def _deps(self, eng, reads, writes):
        waits = {}
        for k in reads:
            self._need(eng, self.lastw.get(k), waits)
        for k in writes:
            self._need(eng, self.lastw.get(k), waits)
            for ev in self.readers.get(k, ()):
                self._need(eng, ev, waits)
        for k, (sem, val) in waits.items():
            self.seen[eng][k] = val
            self.q[eng].append(lambda e, sem=sem, val=val: e.wait_ge(sem, val))

    def _commit(self, ev, reads, writes):
        for k in reads:
            self.readers.setdefault(k, []).append(ev)
        for k in writes:
            self.lastw[k] = ev
            self.readers[k] = []

    def op(self, eng, fn, reads=(), writes=()):
        self._deps(eng, reads, writes)
        self.ecnt[eng] += 1
        sem, val = self.esem[eng], self.ecnt[eng]
        self.q[eng].append(lambda e, fn=fn, sem=sem: fn(e).then_inc(sem, 1))
        self._commit((sem, val), reads, writes)

    def mm(self, fns, reads=(), writes=()):
        eng = "tensor"
        self._deps(eng, reads, writes)
        for fn in fns[:-1]:
            self.q[eng].append(lambda e, fn=fn: fn(e))
        self.ecnt[eng] += 1
        sem, val = self.esem[eng], self.ecnt[eng]
        fn = fns[-1]
        self.q[eng].append(lambda e, fn=fn, sem=sem: fn(e).then_inc(sem, 1))
        self._commit((sem, val), reads, writes)

    def dma(self, eng, out, in_, semkey, reads=(), writes=()):
        self._deps(eng, reads, writes)
        if semkey not in self.dsem:
            self.dsem[semkey] = [self.stack.enter_context(self.nc.semaphore("d_%d" % len(self.dsem))), 0]
        ent = self.dsem[semkey]
        ent[1] += 16
        sem, val = ent[0], ent[1]
        self.q[eng].append(lambda e, out=out, in_=in_, sem=sem: e.dma_start(out=out, in_=in_).then_inc(sem, 16))
        self._commit((sem, val), reads, writes)

    def wait_all_dma(self, eng, keys):
        for k in keys:
            sem, val = self.dsem[k]
            if self.seen[eng].get(id(sem), 0) < val:
                self.seen[eng][id(sem)] = val
                self.q[eng].append(lambda e, sem=sem, val=val: e.wait_ge(sem, val))

    def barrier(self):
        for eng in ENGS:
            for o in ENGS:
                if self.ecnt[o] == 0 or (o == eng == "tensor"):
                    continue
                sem, val = self.esem[o], self.ecnt[o]
                if self.seen[eng].get(id(sem), 0) < val:
                    self.seen[eng][id(sem)] = val
                    self.q[eng].append(lambda e, sem=sem, val=val: e.wait_ge(sem, val))
            for k, (sem, val) in self.dsem.items():
                if val and self.seen[eng].get(id(sem), 0) < val:
                    self.seen[eng][id(sem)] = val
                    self.q[eng].append(lambda e, sem=sem, val=val: e.wait_ge(sem, val))

    def emit(self):
        with self.nc.Block() as block:
            for name in ENGS:
                q = self.q[name]
                if not q:
                    continue

                def body(e, q=q):
                    for th in q:
                        th(e)
                getattr(block, name)(body)


_CONST = {}


def consts():
    if _CONST:
        return _CONST
    n1 = np.arange(128, dtype=np.float64)
    n2 = np.arange(16, dtype=np.float64)
    k1 = np.arange(128, dtype=np.float64)
    th = 2 * np.pi * (16 * n1[:, None, None] + n2[None, :, None]) * (k1[None, None, :] + 0.5) / NFFT
    fw = np.stack([np.cos(th), -np.sin(th)], axis=2)
    gw = (2.0 / NFFT) * np.transpose(fw, (3, 1, 2, 0))
    ph = 2 * np.pi * np.outer(n2, n2) / 16
    eye8 = np.eye(8)
    kC = np.kron(np.cos(ph), eye8)
    kS = np.kron(np.sin(ph), eye8)
    kmats = np.stack([kC, kS, -kS, kC, np.eye(128), np.ones((128, 128))], axis=1)
    _CONST["fw"] = np.ascontiguousarray(fw).astype(NPBF)
    _CONST["gw"] = np.ascontiguousarray(gw).astype(NPBF)
    _CONST["kmats"] = np.ascontiguousarray(kmats).astype(NPBF)
    slopes = np.exp2(-8.0 * np.arange(1, 9, dtype=np.float64) / 8).reshape(2, 4)
    pk = np.arange(128)[:, None]
    pq = np.arange(128)[None, :]
    dt = np.zeros((128, 2, 3, 4, 128))
    for s in range(3):
        rel = np.abs((s - 1) * 128 + pk - pq)
        for kv in range(2):
            for g in range(4):
                dt[:, kv, s, g, :] = np.where(rel <= 128, np.exp(-slopes[kv, g] * rel), 0.0)
    _CONST["dtab"] = dt.astype(NPBF)
    t = np.arange(L, dtype=np.float32)
    tn = t / np.float32(L - 1)
    w = np.float32(2.0 * math.pi) * t / np.float32(L)
    bands = np.linspace(1e-4, 15, 16).astype(np.float32)
    ang = w[:, None] * bands[None, :]
    z = np.concatenate([tn[:, None], np.cos(ang), -np.sin(ang)], axis=-1).astype(np.float32)
    _CONST["zT"] = np.ascontiguousarray(z.T)
    min_decay = math.log(1e-2) / 1.5
    max_decay = math.log(1e-2) / 0.3
    deltas = np.abs(np.linspace(min_decay, max_decay, DH)).astype(np.float32)
    decay = np.exp(-tn[:, None] * deltas[None, :]).astype(np.float32)
    _CONST["decay"] = np.ascontiguousarray(decay.T.reshape(4, 128, L).transpose(1, 0, 2))
    return _CONST


def build(nb=NB, dbg=(), stop=None):
    nc = bass.Bass("TRN2", target_bir_lowering=False)

    def din(name, shape, dt=F32):
        return nc.dram_tensor(name, list(shape), dt, kind="ExternalInput").ap()

    x_d = din("x", [nb, L, DM])
    win_d = din("win", [NCH, 128, 8, 128])
    wout_d = din("wout", [128, 8, DM])
    preg_d = din("preg", [128, 8])
    postg_d = din("postg", [128, DM])
    convw_d = din("convw", [128, 12, 4])
    hd_d = din("hd", [128, 4])
    sink_d = din("sink", [128, 8])
    wf1_d = din("wf1", [33, 64])
    wf2_d = din("wf2", [64, 64])
    wf3_d = din("wf3", [64, 64])
    wf4_d = din("wf4", [64, 1024])
    fb_d = din("fb", [64, 3])
    fr_d = din("fr", [64, 3])
    zT_d = din("zT", [33, L])
    decay_d = din("decay", [128, 4, L])
    fw_d = din("fw", [128, 16, 2, 128], BF16)
    gw_d = din("gw", [128, 16, 2, 128], BF16)
    kmats_d = din("kmats", [128, 6, 128], BF16)
    dtab_d = din("dtab", [128, 2, 3, 4, 128], BF16)
    out_d = nc.dram_tensor("out", [nb, L, DM], F32, kind="ExternalOutput").ap()
    kfs_d = nc.dram_tensor("kfs", [4, 128, 16 * 2 * 128], BF16).ap()
    wsc_d = nc.dram_tensor("wsc", [NCH, 128, 8 * 128], BF16).ap()
    dbg_out = {}

    with ExitStack() as st:
        S = Sched(nc, st)

        def sb(name, shape, dt):
            return st.enter_context(nc.sbuf_tensor("sb_" + name, list(shape), dt))

        kmats = sb("kmats", [128, 6, 128], BF16)
        fw = sb("fw", [128, 16, 2, 128], BF16)
        gw = sb("gw", [128, 16, 2, 128], BF16)
        dtab = sb("dtab", [128, 2, 3, 4, 128], BF16)
        esinkT = sb("esinkT", [128, 8, 128], BF16)
        woutb = sb("woutb", [128, 8, DM], BF16)
        postg = sb("postg", [128, DM], F32)
        preg = sb("preg", [128, 8], F32)
        convw = sb("convw", [128, 12, 4], F32)
        hd = sb("hd", [128, 4], F32)
        sinkt = sb("sinkt", [128, 8], F32)
        small = sb("small", [128, 64], F32)
        kfslot = sb("kfslot", [128, 16, 2, 128], BF16)
        wst = sb("wst", [128, 2, 8, 128], F32)
        wbf = sb("wbf", [128, 2, 8, 128], BF16)
        xs = sb("xs", [128, 2, DM], F32)
        xn = sb("xn", [128, DM], BF16)
        junk = sb("junk", [128, DM], BF16)
        outt = sb("outt", [128, 2, DM], F32)
        ss = sb("ss", [128, 32], F32)
        rstd = sb("rstd", [128, 32], F32)
        hT = sb("hT", [128, 8, L], BF16)
        ymixT = sb("ymixT", [128, 8, L], BF16)
        REG_BF = 24 * 1024
        reg = sb("reg", [128, REG_BF], BF16)

        def carve_bf(off_kb, shape):
            n = int(np.prod(shape[1:]))
            o = off_kb * 512
            ap = reg[:, o:o + n]
            if len(shape) == 3:
                ap = ap.rearrange("p (a b) -> p a b", b=shape[2])
            elif len(shape) == 4:
                ap = ap.rearrange("p (a b c) -> p a b c", b=shape[2], c=shape[3])
            return ap

        def carve_f32(off_kb, shape):
            n = int(np.prod(shape[1:]))
            o = off_kb * 512
            ap = reg[:, o:o + 2 * n].bitcast(F32)
            if len(shape) == 3:
                ap = ap.rearrange("p (a b) -> p a b", b=shape[2])
            return ap

        x0c = carve_bf(0, [128, L])
        x1c = carve_bf(4, [128, L])
        vc = carve_bf(8, [128, L])
        sg = carve_bf(12, [128, L])
        vT = carve_bf(16, [128, 16, 128])
        A_sb = carve_bf(20, [128, 2, 16, 128])
        A2 = carve_bf(28, [128, 2, 16, 128])
        t1 = carve_bf(36, [128, 1024])
        t2 = carve_bf(38, [128, 1024])
        x0c2 = carve_bf(40, [128, L])
        x0bufs = [x0c, x0c2]
        ub = carve_bf(44, [128, L])
        ytok = vT
        Bbig = A_sb
        Zt = A2
        qT = carve_bf(0, [128, 16, 512])
        kT = carve_bf(16, [128, L])
        vTa = carve_bf(20, [128, L])
        vtok = carve_bf(24, [128, 16, 128])
        Pt = carve_bf(28, [128, 2, 3, 512])
        rden = carve_f32(34, [128, 2, 512])

        ps = st.enter_context(nc.psum_tensor("ps", [128, 4096], F32))
        psb = ps[:, :].bitcast(BF16)

        def bank(i, n=1):
            return ps[:, i * 512:(i + n) * 512]

        def bankb(i, n=1):
            return psb[:, i * 1024:(i + n) * 1024]

        def B(*idx):
            return [("B", i) for i in idx]

        def dump(name, ap, keys):
            if name not in dbg:
                return
            d = nc.dram_tensor("dbg_" + name, list(ap.shape), ap.dtype, kind="ExternalOutput").ap()
            dbg_out[name] = d
            S.dma("sync", d, ap, "dbg", reads=keys)

        ident = kmats[:, 4, :]
        ones = kmats[:, 5, :]
        kC = kmats[:, 0, :]
        kS = kmats[:, 1, :]
        kSn = kmats[:, 2, :]
        kCS = kmats[:, 0:2, :].rearrange("p a b -> p (a b)")
        kSnC = kmats[:, 2:4, :].rearrange("p a b -> p (a b)")

        def cload(t_ap, d_ap, key):
            S.dma("sync", t_ap, d_ap, "const", writes=[key])

        cload(kmats[:], kmats_d, "kmats")
        cload(fw[:], fw_d, "fw")
        cload(gw[:], gw_d, "gw")
        cload(dtab[:], dtab_d, "dtab")
        cload(postg[:], postg_d, "postg")
        cload(preg[:], preg_d, "preg")
        cload(convw[:], convw_d, "convw")
        cload(hd[:], hd_d, "hd")
        cload(sinkt[:], sink_d, "sinkt")

        for ch in range(8):
            sl = ch % 2
            S.dma("sync", xs[:, sl, :], wout_d[:, ch, :], ("xs", sl), writes=[("xs", sl)])
            S.op("scalar", lambda e, ch=ch, sl=sl: e.activation(out=woutb[:, ch, :], in_=xs[:, sl, :], func=AF.Copy, scale=0.5),
                 reads=[("xs", sl)], writes=["woutb"])

        S.op("scalar", lambda e: e.activation(out=small[:, 0:8], in_=sinkt[:], func=AF.Exp), reads=["sinkt"], writes=["small_es"])
        S.op("vector", lambda e: e.tensor_scalar_mul(out=small[:, 0:8], in0=small[:, 0:8], scalar1=1.0 / 128),
             reads=["small_es"], writes=["small_es"])
        S.op("vector", lambda e: e.tensor_copy(out=esinkT[:], in_=small[:, 0:8].unsqueeze(2).to_broadcast([128, 8, 128])),
             reads=["small_es"], writes=["esinkT"])

        def fft_forward(src, src_keys, on_half):
            fns = []
            for a in range(16):
                fns.append(lambda e, a=a: e.transpose(out=bankb(4, 2)[:, a * 128:(a + 1) * 128], in_=src[:, a:L:16], identity=ident))
            S.mm(fns, reads=list(src_keys) + ["kmats"], writes=B(4, 5))
            S.op("scalar", lambda e: e.activation(out=vT.rearrange("p a c -> p (a c)"), in_=bankb(4, 2), func=AF.Copy),
                 reads=B(4, 5), writes=["vT"])
            yield
            for ri in range(2):
                bk = 4
                fns = []
                for a in range(16):
                    fns.append(lambda e, a=a, ri=ri, bk=bk: e.matmul(out=bank(bk, 4)[:, a * 128:(a + 1) * 128], lhsT=fw[:, a, ri, :],
                                                                      rhs=vT[:, a, :], start=True, stop=True))
                S.mm(fns, reads=["vT", "fw"], writes=B(bk, bk + 1, bk + 2, bk + 3))
                srcv = bank(bk, 4).rearrange("p (a g c) -> p a g c", g=16, c=8)
                dstv = A_sb[:, ri, :, :].rearrange("p g (a c) -> p a g c", c=8)
                if ri == 0:
                    S.op("scalar", lambda e, srcv=srcv, dstv=dstv: e.activation(out=dstv, in_=srcv, func=AF.Copy),
                         reads=B(bk, bk + 1, bk + 2, bk + 3), writes=[("A_sb", ri)])
                else:
                    S.op("vector", lambda e, srcv=srcv, dstv=dstv: e.tensor_copy(out=dstv, in_=srcv),
                         reads=B(bk, bk + 1, bk + 2, bk + 3), writes=[("A_sb", ri)])
                yield
            fns = []
            for ri in range(2):
                for g in range(16):
                    o = (ri * 16 + g) * 128
                    fns.append(lambda e, g=g, ri=ri, o=o: e.transpose(out=bankb(4, 4)[:, o:o + 128],
                                                                       in_=A_sb[:, ri, g, :], identity=ident))
            S.mm(fns, reads=[("A_sb", 0), ("A_sb", 1), "kmats"], writes=B(4, 5, 6, 7))
            pv = bankb(4, 4).rearrange("p (r g k) -> p r g k", r=2, k=128)
            S.op("scalar", lambda e: e.activation(out=A2[:, :, 0:8, :], in_=pv[:, :, 0:8, :], func=AF.Copy),
                 reads=B(4, 5, 6, 7), writes=[("A2", 0)])
            S.op("vector", lambda e: e.tensor_copy(out=A2[:, :, 8:16, :], in_=pv[:, :, 8:16, :]),
                 reads=B(4, 5, 6, 7), writes=[("A2", 1)])
            yield
            for h in range(2):
                fns = []
                for q4 in range(2):
                    g0 = h * 8 + q4 * 4
                    ar = A2[:, 0, g0:g0 + 4, :].rearrange("p g k -> p (g k)")
                    ai = A2[:, 1, g0:g0 + 4, :].rearrange("p g k -> p (g k)")
                    xr = bank(4 + q4)
                    xi = bank(6 + q4)
                    fns.append(lambda e, ar=ar, xr=xr: e.matmul(out=xr, lhsT=kC, rhs=ar, start=True, stop=False))
                    fns.append(lambda e, ai=ai, xr=xr: e.matmul(out=xr, lhsT=kS, rhs=ai, start=False, stop=True))
                    fns.append(lambda e, ai=ai, xi=xi: e.matmul(out=xi, lhsT=kC, rhs=ai, start=True, stop=False))
                    fns.append(lambda e, ar=ar, xi=xi: e.matmul(out=xi, lhsT=kSn, rhs=ar, start=False, stop=True))
                S.mm(fns, reads=[("A2", h), "kmats"], writes=B(4, 5, 6, 7))
                on_half(h)
                yield

        def run_gen(g):
            for _ in g:
                pass

        def interleave(ga, gb):
            alive = [ga, gb]
            while alive:
                for g in list(alive):
                    try:
                        next(g)
                    except StopIteration:
                        alive.remove(g)

        XR = bank(4, 2).rearrange("p (g k) -> p g k", k=128)
        XI = bank(6, 2).rearrange("p (g k) -> p g k", k=128)

        hTf = hT[:, :, :].rearrange("p a b -> p (a b)")
        ymf = ymixT[:, :, :].rearrange("p a b -> p (a b)")

        def f32view(flat, off_kb, n):
            return flat[:, off_kb * 512: off_kb * 512 + 2 * n].bitcast(F32)

        zT = f32view(hTf, 0, L)
        hA = f32view(hTf, 8, L)
        hB = f32view(hTf, 16, L)
        tmpf = f32view(hTf, 24, L)
        kfr = f32view(ymf, 0, L)
        kbr = f32view(ymf, 8, L)
        dec = f32view(ymf, 16, L)
        wf4 = f32view(ymf, 24, 1024)
        wsm = f32view(ymf, 28, 256)
        fbr = small[:, 8:11]
        frr = small[:, 11:14]
        frb = small[:, 14:17]
        Hfw = outt[:, 0, :].bitcast(BF16).rearrange("p (g r k) -> p g r k", r=2, k=128)
        Hfw2 = outt[:, 1, :].bitcast(BF16).rearrange("p (g r k) -> p g r k", r=2, k=128)

        S.barrier()
        S.dma("sync", zT[0:33, :], zT_d, "const", writes=["zT"])
        S.dma("sync", wsm[0:33, 0:64], wf1_d, "const", writes=["wsm"])
        S.dma("sync", wsm[0:64, 64:128], wf2_d, "const", writes=["wsm"])
        S.dma("sync", wsm[0:64, 128:192], wf3_d, "const", writes=["wsm"])
        S.dma("sync", wf4[0:64, :], wf4_d, "const", writes=["wf4"])
        S.dma("sync", fbr[0:64, :], fb_d, "const", writes=["fbr"])
        S.dma("sync", frr[0:64, :], fr_d, "const", writes=["frr"])
        S.op("vector", lambda e: e.tensor_tensor(out=frb[0:64, :], in0=frr[0:64, :], in1=fbr[0:64, :], op=ALU.mult),
             reads=["fbr", "frr"], writes=["frb"])

        S.op("vector", lambda e: e.tensor_scalar_mul(out=frb[0:64, :], in0=frb[0:64, :], scalar1=1.0 / 9), reads=["frb"], writes=["frb"])
        S.op("vector", lambda e: e.tensor_scalar_mul(out=frr[0:64, :], in0=frr[0:64, :], scalar1=1.0 / 9), reads=["frr"], writes=["frr"])

        def sin_layer(li, lhsT, kdim, src, dst):
            for nchk in range(4):
                S.mm([lambda e, nchk=nchk: e.matmul(out=bank(nchk)[0:64, :], lhsT=lhsT, rhs=src[0:kdim, nchk * 512:(nchk + 1) * 512],
                                                     start=True, stop=True)],
                     reads=["wsm", ("h", li)], writes=B(nchk))
            S.op("scalar", lambda e: e.activation(out=dst[0:64, :], in_=bank(0, 4)[0:64, :], func=AF.Sin,
                                                   scale=frr[0:64, li:li + 1], bias=frb[0:64, li:li + 1]),
                 reads=B(0, 1, 2, 3) + ["frr", "frb"], writes=[("h", li + 1)])
            for _ in range(2):
                S.op("vector", lambda e: e.tensor_tensor(out=tmpf[0:64, :], in0=dst[0:64, :], in1=dst[0:64, :], op=ALU.mult),
                     reads=[("h", li + 1)], writes=["tmpf"])
                S.op("vector", lambda e: e.tensor_scalar(out=tmpf[0:64, :], in0=tmpf[0:64, :], scalar1=-4.0, scalar2=3.0,
                                                         op0=ALU.mult, op1=ALU.add), reads=["tmpf"], writes=["tmpf"])
                S.op("vector", lambda e: e.tensor_tensor(out=dst[0:64, :], in0=dst[0:64, :], in1=tmpf[0:64, :], op=ALU.mult),
                     reads=["tmpf", ("h", li + 1)], writes=[("h", li + 1)])

        S.lastw[("h", 0)] = S.lastw.get("zT")
        sin_layer(0, wsm[0:33, 0:64], 33, zT, hA)
        sin_layer(1, wsm[0:64, 64:128], 64, hA, hB)
        sin_layer(2, wsm[0:64, 128:192], 64, hB, hA)
        h3 = hA
        dump("h3", h3[0:64, :], [("h", 3)])

        def precast(chs):
            for ch in chs:
                sl = ch % 2
                S.dma("sync", wst[:, sl, :, :], win_d[ch], ("wst", sl), writes=[("wst", sl)])
                S.op("gpsimd", lambda e, sl=sl: e.tensor_tensor(out=wbf[:, sl, :, :], in0=wst[:, sl, :, :],
                                                                 in1=preg[:, :].unsqueeze(2).to_broadcast([128, 8, 128]), op=ALU.mult),
                     reads=[("wst", sl), "preg"], writes=[("wbf", sl)])
                S.dma("sync", wsc_d[ch], wbf[:, sl, :, :].rearrange("p a b -> p (a b)"), ("wsc_st", sl), reads=[("wbf", sl)], writes=[("wsc", ch)])

        for cc in range(4):
            precast(range(cc * 7, min(NCH, cc * 7 + 7)))
            S.dma("sync", dec[:, :], decay_d[:, cc, :], "dec", writes=["dec"])
            for fb_i, dstk in ((0, kfr), (1, kbr)):
                col = fb_i * 512 + cc * 128
                for nchk in range(4):
                    S.mm([lambda e, nchk=nchk, col=col: e.matmul(out=bank(nchk), lhsT=wf4[0:64, col:col + 128],
                                                                  rhs=h3[0:64, nchk * 512:(nchk + 1) * 512], start=True, stop=True)],
                         reads=["wf4", ("h", 3)], writes=B(nchk))
                S.op("vector", lambda e, dstk=dstk: e.tensor_tensor(out=dstk[:, :], in0=bank(0, 4), in1=dec[:, :], op=ALU.mult),
                     reads=B(0, 1, 2, 3) + ["dec"], writes=[("kraw", fb_i)])
            S.op("gpsimd", lambda e: e.memset(kbr[:, 0:1], 0.0), reads=[], writes=[("kraw", 1)])
            S.op("scalar", lambda e: e.activation(out=tmpf[:, :], in_=kfr[:, :], func=AF.Square), reads=[("kraw", 0)], writes=["tmpf"])
            S.op("vector", lambda e: e.reduce_sum(out=small[:, 20:21], in_=tmpf[:, :], axis=mybir.AxisListType.X), reads=["tmpf"], writes=["ssq0"])
            S.op("scalar", lambda e: e.activation(out=tmpf[:, :], in_=kbr[:, :], func=AF.Square), reads=[("kraw", 1)], writes=["tmpf"])
            S.op("vector", lambda e: e.reduce_sum(out=small[:, 21:22], in_=tmpf[:, :], axis=mybir.AxisListType.X), reads=["tmpf"], writes=["ssq1"])
            S.op("vector", lambda e: e.scalar_tensor_tensor(out=small[:, 22:23], in0=small[:, 20:21], scalar=1e-12, in1=small[:, 21:22],
                                                            op0=ALU.add, op1=ALU.add), reads=["ssq0", "ssq1"], writes=["ssq"])
            S.op("scalar", lambda e: e.activation(out=small[:, 22:23], in_=small[:, 22:23], func=AF.Ln), reads=["ssq"], writes=["ssq"])
            S.op("scalar", lambda e: e.activation(out=small[:, 23:24], in_=small[:, 22:23], func=AF.Exp, scale=-0.5), reads=["ssq"], writes=["krs"])
            S.op("vector", lambda e: e.tensor_scalar(out=x0c, in0=kfr[:, :], scalar1=small[:, 23:24], scalar2=None, op0=ALU.mult),
                 reads=[("kraw", 0), "krs"], writes=["x0c"])
            S.op("vector", lambda e, cc=cc: e.scalar_tensor_tensor(out=x0c[:, 0:1], in0=kfr[:, 0:1], scalar=small[:, 23:24], in1=hd[:, cc:cc + 1],
                                                                    op0=ALU.mult, op1=ALU.add), reads=[("kraw", 0), "krs", "hd", "x0c"], writes=["x0c"])
            S.op("vector", lambda e: e.tensor_scalar(out=x1c, in0=kbr[:, :], scalar1=small[:, 23:24], scalar2=None, op0=ALU.mult),
                 reads=[("kraw", 1), "krs"], writes=["x1c"])
            if cc == 0:
                dump("kf0", x0c, ["x0c"])
                dump("kb0", x1c, ["x1c"])

            def keep_f(h):
                dst = Hfw if h == 0 else Hfw2
                S.op("scalar", lambda e, dst=dst: e.activation(out=dst[:, :, 0, :], in_=XR, func=AF.Copy), reads=B(4, 5), writes=[("Hf", h)])
                S.op("vector", lambda e, dst=dst: e.tensor_copy(out=dst[:, :, 1, :], in_=XI), reads=B(6, 7), writes=[("Hf", h)])

            def keep_b(h):
                dst = Hfw if h == 0 else Hfw2
                S.op("vector", lambda e, dst=dst, h=h: e.tensor_tensor(out=kfslot[:, h * 8:(h + 1) * 8, 0, :], in0=XR, in1=dst[:, :, 0, :], op=ALU.add),
                     reads=B(4, 5) + [("Hf", h)], writes=[("kfslot", h)])
                S.op("vector", lambda e, dst=dst, h=h: e.tensor_tensor(out=kfslot[:, h * 8:(h + 1) * 8, 1, :], in0=dst[:, :, 1, :], in1=XI, op=ALU.subtract),
                     reads=B(6, 7) + [("Hf", h)], writes=[("kfslot", h)])

            run_gen(fft_forward(x0c, ["x0c"], keep_f))
            run_gen(fft_forward(x1c, ["x1c"], keep_b))
            S.dma("sync", kfs_d[cc], kfslot[:, :, :, :].rearrange("p g r k -> p (g r k)"), ("kfs", cc),
                  reads=[("kfslot", 0), ("kfslot", 1)], writes=[("kfs", cc)])
            if cc == 0:
                dump("Kf0", kfslot[:, :, :, :], [("kfslot", 0), ("kfslot", 1)])

        S.barrier()

        wcount = [0]

        def load_w(ch):
            sl = wcount[0] % 2
            wcount[0] += 1
            S.dma("sync", wbf[:, sl, :, :].rearrange("p a b -> p (a b)"), wsc_d[ch], ("wbf", sl), reads=[("wsc", ch)], writes=[("wbf", sl)])
            return sl

        ucount = [0]

        def inproj_gen(ch, bk):
            sl = load_w(ch)
            for tcn in range(4):
                fns = []
                for kc in range(8):
                    fns.append(lambda e, kc=kc, tcn=tcn, sl=sl, bk=bk: e.matmul(out=bank(bk + tcn), lhsT=wbf[:, sl, kc, :],
                                                                                  rhs=hT[:, kc, tcn * 512:(tcn + 1) * 512],
                                                                                  start=(kc == 0), stop=(kc == 7)))
                S.mm(fns, reads=[("wbf", sl), "hT"], writes=B(bk + tcn))
                yield

        def inproj(ch):
            bk = 4 * (ucount[0] % 2)
            ucount[0] += 1
            run_gen(inproj_gen(ch, bk))
            return bk

        def conv_evac(bk, dst, dkey, ci):
            U = bank(bk, 4)
            ks = B(bk, bk + 1, bk + 2, bk + 3)
            S.op("scalar", lambda e: e.activation(out=ub, in_=U, func=AF.Copy), reads=ks, writes=["ub"])
            S.op("vector", lambda e: e.tensor_scalar(out=dst, in0=ub, scalar1=convw[:, ci, 1:2], scalar2=convw[:, ci, 3:4],
                                                     op0=ALU.mult, op1=ALU.add), reads=["ub", "convw"], writes=[dkey])
            S.op("vector", lambda e: e.scalar_tensor_tensor(out=dst[:, 1:L], in0=ub[:, 0:L - 1], scalar=convw[:, ci, 0:1], in1=dst[:, 1:L],
                                                            op0=ALU.mult, op1=ALU.add), reads=["ub", "convw", dkey], writes=[dkey])
            S.op("vector", lambda e: e.scalar_tensor_tensor(out=dst[:, 0:L - 1], in0=ub[:, 1:L], scalar=convw[:, ci, 2:3], in1=dst[:, 0:L - 1],
                                                            op0=ALU.mult, op1=ALU.add), reads=["ub", "convw", dkey], writes=[dkey])

        def silu_evac(bk, dst, dkey):
            U = bank(bk, 4)
            ks = B(bk, bk + 1, bk + 2, bk + 3)
            S.op("scalar", lambda e: e.activation(out=dst, in_=U, func=AF.Tanh, scale=0.5), reads=ks, writes=[dkey])
            S.op("vector", lambda e: e.scalar_tensor_tensor(out=dst, in0=dst, scalar=1.0, in1=U, op0=ALU.add, op1=ALU.mult),
                 reads=ks + [dkey], writes=[dkey])

        for b in range(nb if stop != "setup" else 0):
            for tt in range(16):
                sl = tt % 2
                S.dma("sync", xs[:, sl, :], x_d[b, tt * 128:(tt + 1) * 128, :], ("xs", sl), writes=[("xs", sl)])
                S.op("scalar", lambda e, sl=sl, tt=tt: e.activation(out=junk[:, :], in_=xs[:, sl, :], func=AF.Square, scale=1.0 / 32),
                     reads=[("xs", sl)], writes=["junk"])
                S.op("vector", lambda e, tt=tt: e.reduce_sum(out=ss[:, tt:tt + 1], in_=junk[:, :], axis=mybir.AxisListType.X),
                     reads=["junk"], writes=[("ss", tt)])
                S.op("vector", lambda e, tt=tt: e.tensor_scalar_add(out=ss[:, tt:tt + 1], in0=ss[:, tt:tt + 1], scalar1=EPS),
                     reads=[("ss", tt)], writes=[("ss", tt)])
                S.op("scalar", lambda e, tt=tt: e.activation(out=rstd[:, tt:tt + 1], in_=ss[:, tt:tt + 1], func=AF.Ln),
                     reads=[("ss", tt)], writes=[("rstd", tt)])
                S.op("scalar", lambda e, tt=tt: e.activation(out=rstd[:, tt:tt + 1], in_=rstd[:, tt:tt + 1], func=AF.Exp, scale=-0.5),
                     reads=[("rstd", tt)], writes=[("rstd", tt)])
                S.op("vector", lambda e, sl=sl, tt=tt: e.tensor_scalar(out=xn[:, :], in0=xs[:, sl, :], scalar1=rstd[:, tt:tt + 1], scalar2=None,
                                                                       op0=ALU.mult), reads=[("xs", sl), ("rstd", tt)], writes=["xn"])
                fns = []
                for kc in range(8):
                    fns.append(lambda e, kc=kc: e.transpose(out=bankb(3)[:, kc * 128:(kc + 1) * 128], in_=xn[:, kc * 128:(kc + 1) * 128], identity=ident))
                S.mm(fns, reads=["xn", "kmats"], writes=B(3))
                S.op("vector", lambda e, tt=tt: e.tensor_copy(out=hT[:, :, tt * 128:(tt + 1) * 128],
                                                              in_=bankb(3).rearrange("p (a b) -> p a b", b=128)), reads=B(3), writes=["hT"])
            if b == 0:
                dump("hT", hT[:, :, :], ["hT"])

            if stop == "phase1":
                continue
            def hy_inproj(cc):
                base = cc * 4
                x0b = x0bufs[cc % 2]
                x0k = ("x0c", cc % 2)
                yield from inproj_gen(base + 1, 0)
                conv_evac(0, x1c, "x1c", 1 * 4 + cc)
                yield from inproj_gen(base + 2, 0)
                conv_evac(0, vc, "vc", 2 * 4 + cc)
                S.op("gpsimd", lambda e: e.tensor_tensor(out=vc, in0=vc, in1=x1c, op=ALU.mult), reads=["vc", "x1c"], writes=["vc"])
                yield from inproj_gen(base + 3, 0)
                silu_evac(0, sg, "sg")
                yield from inproj_gen(base + 0, 0)
                conv_evac(0, x0b, x0k, 0 * 4 + cc)
                S.op("gpsimd", lambda e, x0b=x0b: e.tensor_tensor(out=x0b, in0=x0b, in1=sg, op=ALU.mult), reads=[x0k, "sg"], writes=[x0k])

            def hy_fft(cc):
                x0b = x0bufs[cc % 2]
                x0k = ("x0c", cc % 2)
                S.dma("sync", kfslot[:, :, :, :].rearrange("p g r k -> p (g r k)"), kfs_d[cc], "kfslot_ld",
                      reads=[("kfs", cc)], writes=[("kfslot", 0), ("kfslot", 1)])

                def mulk(h):
                    kr = kfslot[:, h * 8:(h + 1) * 8, 0, :]
                    ki = kfslot[:, h * 8:(h + 1) * 8, 1, :]
                    t1v = t1.rearrange("p (g k) -> p g k", k=128)
                    t2v = t2.rearrange("p (g k) -> p g k", k=128)
                    zr = Zt[:, 0, h * 8:(h + 1) * 8, :]
                    zi = Zt[:, 1, h * 8:(h + 1) * 8, :]
                    kk = [("kfslot", h)]
                    S.op("vector", lambda e: e.tensor_tensor(out=t1v, in0=XR, in1=kr, op=ALU.mult), reads=B(4, 5) + kk, writes=["t1"])
                    S.op("vector", lambda e: e.tensor_tensor(out=t2v, in0=XI, in1=ki, op=ALU.mult), reads=B(6, 7) + kk, writes=["t2"])
                    S.op("gpsimd", lambda e: e.tensor_tensor(out=zr, in0=t1v, in1=t2v, op=ALU.subtract), reads=["t1", "t2", ("A2", h)], writes=[("A2", h)])
                    S.op("vector", lambda e: e.tensor_tensor(out=t1v, in0=XR, in1=ki, op=ALU.mult), reads=B(4, 5) + kk + ["t1"], writes=["t1"])
                    S.op("vector", lambda e: e.tensor_tensor(out=t2v, in0=XI, in1=kr, op=ALU.mult), reads=B(6, 7) + kk + ["t2"], writes=["t2"])
                    S.op("gpsimd", lambda e: e.tensor_tensor(out=zi, in0=t1v, in1=t2v, op=ALU.add), reads=["t1", "t2", ("A2", h)], writes=[("A2", h)])

                yield from fft_forward(vc, ["vc"], mulk)
                for h in range(2):
                    fns = []
                    for gi in range(8):
                        g = h * 8 + gi
                        o1 = bank(4, 4)[:, gi * 256:gi * 256 + 128]
                        o2 = bank(4, 4)[:, gi * 256 + 128:gi * 256 + 256]
                        fns.append(lambda e, g=g, o1=o1: e.matmul(out=o1, lhsT=Zt[:, 0, g, :], rhs=kC, start=True, stop=False))
                        fns.append(lambda e, g=g, o1=o1: e.matmul(out=o1, lhsT=Zt[:, 1, g, :], rhs=kSn, start=False, stop=True))
                        fns.append(lambda e, g=g, o2=o2: e.matmul(out=o2, lhsT=Zt[:, 0, g, :], rhs=kS, start=True, stop=False))
                        fns.append(lambda e, g=g, o2=o2: e.matmul(out=o2, lhsT=Zt[:, 1, g, :], rhs=kC, start=False, stop=True))
                    S.mm(fns, reads=[("A2", h), "kmats"], writes=B(4, 5, 6, 7))
                    for gi in range(8):
                        g = h * 8 + gi
                        for ri in range(2):
                            srcv = bank(4, 4)[:, gi * 256 + ri * 128: gi * 256 + ri * 128 + 128].rearrange("p (a c) -> p a c", c=8)
                            dstv = Bbig[:, ri, :, g * 8:(g + 1) * 8]
                            S.op("scalar", lambda e, dstv=dstv, srcv=srcv: e.activation(out=dstv, in_=srcv, func=AF.Copy),
                                 reads=B(4 + gi // 2), writes=[("A_sb", ri)])
                    yield
                fns = []
                for a in range(16):
                    o = bank(4, 4)[:, a * 128:(a + 1) * 128]
                    fns.append(lambda e, a=a, o=o: e.matmul(out=o, lhsT=gw[:, a, 0, :], rhs=Bbig[:, 0, a, :], start=True, stop=False))
                    fns.append(lambda e, a=a, o=o: e.matmul(out=o, lhsT=gw[:, a, 1, :], rhs=Bbig[:, 1, a, :], start=False, stop=True))
                S.mm(fns, reads=[("A_sb", 0), ("A_sb", 1), "gw"], writes=B(4, 5, 6, 7))
                S.op("scalar", lambda e: e.activation(out=ytok.rearrange("p a c -> p (a c)"), in_=bank(4, 4), func=AF.Copy),
                     reads=B(4, 5, 6, 7), writes=["vT"])
                yield
                fns = []
                for a in range(16):
                    fns.append(lambda e, a=a: e.transpose(out=bankb(4, 2)[:, a * 128:(a + 1) * 128], in_=ytok[:, a, :], identity=ident))
                S.mm(fns, reads=["vT", "kmats"], writes=B(4, 5))
                S.op("vector", lambda e, cc=cc, x0b=x0b: e.tensor_tensor(out=ymixT[:, cc, :].rearrange("p (q a) -> p q a", a=16),
                                                                in0=bankb(4, 2).rearrange("p (a q) -> p q a", q=128),
                                                                in1=x0b.rearrange("p (q a) -> p q a", a=16), op=ALU.mult),
                     reads=B(4, 5) + [x0k], writes=[("ymixT", cc)])
                if b == 0 and cc == 0:
                    dump("ym0", ymixT[:, 0, :], [("ymixT", 0)])
                yield

            ncc = 4 if stop not in ("hy1", "hyA", "hyB", "hyC", "hyD") else 1
            run_gen(hy_inproj(0))
            for cc in range(ncc):
                if cc + 1 < ncc:
                    interleave(hy_fft(cc), hy_inproj(cc + 1))
                else:
                    run_gen(hy_fft(cc))
            if stop in ("hyena", "hy1", "hyA", "hyB", "hyC", "hyD"):
                continue
            S.barrier()
            for j in range(4):
                bk = inproj(16 + j)
                ks = B(bk, bk + 1, bk + 2, bk + 3)
                S.op("scalar", lambda e, j=j, bk=bk: e.activation(out=qT[:, :, j * 128:(j + 1) * 128], in_=bank(bk, 4).rearrange("p (i q) -> p i q", q=128), func=AF.Copy), reads=ks, writes=["qT"])
            bk = inproj(20)
            S.op("vector", lambda e, bk=bk: e.tensor_copy(out=kT, in_=bank(bk, 4)), reads=B(bk, bk + 1, bk + 2, bk + 3), writes=["kT"])
            bk = inproj(21)
            S.op("scalar", lambda e, bk=bk: e.activation(out=vTa, in_=bank(bk, 4), func=AF.Copy), reads=B(bk, bk + 1, bk + 2, bk + 3), writes=["vTa"])
            fns = []
            for tt in range(16):
                fns.append(lambda e, tt=tt: e.transpose(out=bankb(3)[:, 0:128] if False else bankb(6, 2)[:, tt * 128:(tt + 1) * 128],
                                                        in_=vTa[:, tt * 128:(tt + 1) * 128], identity=ident))
            S.mm(fns, reads=["vTa", "kmats"], writes=B(6, 7))
            S.op("vector", lambda e: e.tensor_copy(out=vtok.rearrange("p a c -> p (a c)"), in_=bankb(6, 2)), reads=B(6, 7), writes=["vtok"])
            units = [(i, kv) for i in range(16) for kv in range(2)]

            def unit_info(u):
                i, kv = units[u]
                kbs = [kb for kb in (i - 1, i, i + 1) if 0 <= kb < 16]
                return i, kv, kbs, len(kbs), kbs[0] - (i - 1), slice(kv * 64, (kv + 1) * 64), (0 if u % 2 == 0 else 4), u % 2

            def unit_a(u):
                i, kv, kbs, nk, s0, rows, sb0, psl = unit_info(u)
                fns = []
                for si, kb in enumerate(kbs):
                    fns.append(lambda e, si=si, kb=kb, rows=rows, sb0=sb0, i=i: e.matmul(
                        out=bank(sb0 + si), lhsT=kT[rows, kb * 128:(kb + 1) * 128],
                        rhs=qT[rows, i, :], start=True, stop=True))
                sk = B(*[sb0 + si for si in range(nk)])
                S.mm(fns, reads=["kT", "qT"], writes=sk)
                S.op("scalar", lambda e, sb0=sb0, nk=nk, psl=psl: e.activation(out=Pt[:, psl, 0:nk, :].rearrange("p s n -> p (s n)"),
                                                                                in_=bank(sb0, nk), func=AF.Exp, scale=0.125),
                     reads=sk, writes=[("P", psl)])
                S.op("vector", lambda e, nk=nk, psl=psl, kv=kv, s0=s0: e.tensor_tensor(
                    out=Pt[:, psl, 0:nk, :].rearrange("p s (g q) -> p s g q", q=128), in0=Pt[:, psl, 0:nk, :].rearrange("p s (g q) -> p s g q", q=128),
                    in1=dtab[:, kv, s0:s0 + nk, :, :], op=ALU.mult), reads=[("P", psl), "dtab"], writes=[("P", psl)])

            def unit_b(u):
                i, kv, kbs, nk, s0, rows, sb0, psl = unit_info(u)
                fns = []
                for si, kb in enumerate(kbs):
                    fns.append(lambda e, si=si, kb=kb, psl=psl: e.matmul(out=bank(3), lhsT=vtok[:, kb, :], rhs=Pt[:, psl, si, :],
                                                                         start=(si == 0), stop=(si == nk - 1)))
                for si, kb in enumerate(kbs):
                    fns.append(lambda e, si=si, psl=psl: e.matmul(out=bank(7), lhsT=ones, rhs=Pt[:, psl, si, :], start=(si == 0), stop=False))
                fns.append(lambda e, kv=kv: e.matmul(out=bank(7), lhsT=ones, rhs=esinkT[:, kv * 4:(kv + 1) * 4, :].rearrange("p g q -> p (g q)"),
                                                     start=False, stop=True))
                S.mm(fns, reads=[("P", psl), "vtok", "kmats", "esinkT"], writes=B(3, 7))
                S.op("scalar", lambda e, rows=rows, psl=psl: e.activation(out=rden[rows, psl, :], in_=bank(7)[rows, :], func=AF.Ln),
                     reads=B(7), writes=[("rden", psl)])
                S.op("scalar", lambda e, rows=rows, psl=psl: e.activation(out=rden[rows, psl, :], in_=rden[rows, psl, :], func=AF.Exp, scale=-1.0),
                     reads=[("rden", psl)], writes=[("rden", psl)])
                S.op("vector", lambda e, rows=rows, psl=psl, i=i: e.tensor_tensor(
                    out=ymixT[rows, 4:8, i * 128:(i + 1) * 128], in0=bank(3)[rows, :].rearrange("p (g q) -> p g q", q=128),
                    in1=rden[rows, psl, :].rearrange("p (g q) -> p g q", q=128), op=ALU.mult),
                    reads=B(3) + [("rden", psl)], writes=[("ymixT", "a")])

            unit_a(0)
            for u in range(len(units)):
                if u + 1 < len(units):
                    unit_a(u + 1)
                unit_b(u)
            for j in range(4):
                bk = inproj(22 + j)
                silu_evac(bk, sg, "sg")
                S.op("gpsimd", lambda e, j=j: e.tensor_tensor(out=ymixT[:, 4 + j, :], in0=ymixT[:, 4 + j, :], in1=sg, op=ALU.mult),
                     reads=["sg", ("ymixT", "a")], writes=[("ymixT", "a")])
            if b == 0:
                dump("ymixT", ymixT[:, :, :], [("ymixT", c) for c in (0, 1, 2, 3, "a")])
            if stop == "attn":
                continue
            S.barrier()
            ymk = [("ymixT", c) for c in (0, 1, 2, 3, "a")]
            for tt in range(16):
                sl = tt % 2
                S.dma("sync", xs[:, sl, :], x_d[b, tt * 128:(tt + 1) * 128, :], ("xs", sl), writes=[("xs", sl)])
                yb = 0 if tt % 2 == 0 else 4
                for hf in range(2):
                    fns = []
                    for ch in range(8):
                        fns.append(lambda e, ch=ch, hf=hf, tt=tt, yb=yb: e.matmul(out=bank(yb + hf), lhsT=ymixT[:, ch, tt * 128:(tt + 1) * 128],
                                                                                    rhs=woutb[:, ch, hf * 512:(hf + 1) * 512],
                                                                                    start=(ch == 0), stop=(ch == 7)))
                    S.mm(fns, reads=ymk + ["woutb"], writes=B(yb + hf))
                yk = B(yb, yb + 1)
                c0 = 16 + tt
                S.op("scalar", lambda e, yb=yb, c0=c0: e.activation(out=junk[:, :], in_=bank(yb, 2), func=AF.Square, scale=1.0 / 32),
                     reads=yk, writes=["junk"])
                S.op("vector", lambda e, c0=c0: e.reduce_sum(out=ss[:, c0:c0 + 1], in_=junk[:, :], axis=mybir.AxisListType.X),
                     reads=["junk"], writes=[("ss", c0)])
                S.op("vector", lambda e, c0=c0: e.tensor_scalar_add(out=ss[:, c0:c0 + 1], in0=ss[:, c0:c0 + 1], scalar1=EPS),
                     reads=[("ss", c0)], writes=[("ss", c0)])
                S.op("scalar", lambda e, c0=c0: e.activation(out=rstd[:, c0:c0 + 1], in_=ss[:, c0:c0 + 1], func=AF.Ln),
                     reads=[("ss", c0)], writes=[("rstd", c0)])
                S.op("scalar", lambda e, c0=c0: e.activation(out=rstd[:, c0:c0 + 1], in_=rstd[:, c0:c0 + 1], func=AF.Exp, scale=-0.5),
                     reads=[("rstd", c0)], writes=[("rstd", c0)])
                S.op("vector", lambda e, yb=yb, c0=c0, sl=sl: e.scalar_tensor_tensor(out=outt[:, sl, :], in0=bank(yb, 2), scalar=rstd[:, c0:c0 + 1],
                                                                                     in1=postg[:, :], op0=ALU.mult, op1=ALU.mult),
                     reads=yk + [("rstd", c0), "postg"], writes=[("outt", sl)])
                S.op("gpsimd", lambda e, sl=sl: e.tensor_tensor(out=outt[:, sl, :], in0=outt[:, sl, :], in1=xs[:, sl, :], op=ALU.add),
                     reads=[("outt", sl), ("xs", sl)], writes=[("outt", sl)])
                S.dma("sync", out_d[b, tt * 128:(tt + 1) * 128, :], outt[:, sl, :], ("outst", sl), reads=[("outt", sl)])

        S.barrier()
        S.wait_all_dma("sync", [k for k in (("outst", 0), ("outst", 1), "dbg") if k in S.dsem])
        S.emit()
    return nc, dbg_out


def col_chunks():
    o_hg, o_q, o_k, o_v, o_ag = 1536, 2048, 2560, 2688, 2816
    ch = []
    for cc in range(4):
        for base in (0, 512, 1024, o_hg):
            ch.append(np.arange(base + cc * 128, base + (cc + 1) * 128))
    for j in range(4):
        ch.append(np.concatenate([o_q + j * 64 + np.arange(64), o_q + (j + 4) * 64 + np.arange(64)]))
    ch.append(o_k + np.arange(128))
    ch.append(o_v + np.arange(128))
    for j in range(4):
        ch.append(np.concatenate([o_ag + j * 64 + np.arange(64), o_ag + (j + 4) * 64 + np.arange(64)]))
    return ch


def mix_rows():
    rows = [np.arange(cc * 128, (cc + 1) * 128) for cc in range(4)]
    for j in range(4):
        rows.append(np.concatenate([512 + j * 64 + np.arange(64), 512 + (j + 4) * 64 + np.arange(64)]))
    return rows


def prep_shared(inp):
    c = consts()
    f = lambda a: np.ascontiguousarray(np.asarray(a, dtype=np.float32))
    w_in = f(inp["w_in"])[0]
    chs = col_chunks()
    win = np.stack([w_in[:, cols].reshape(8, 128, 128).transpose(1, 0, 2) for cols in chs], axis=0)
    w_out = f(inp["w_out"])[0]
    wout = np.stack([w_out[r, :] for r in mix_rows()], axis=1)
    w_short = f(inp["w_short"])[0]
    b_short = f(inp["b_short"])[0]
    convw = np.zeros((128, 12, 4), np.float32)
    for ty in range(3):
        for cc in range(4):
            idx = ty * 512 + cc * 128 + np.arange(128)
            convw[:, ty * 4 + cc, 0:3] = w_short[:, idx].T
            convw[:, ty * 4 + cc, 3] = b_short[idx]
    sink = f(inp["attn_sink"])[0]
    d = dict(
        win=np.ascontiguousarray(win), wout=np.ascontiguousarray(wout),
        preg=np.ascontiguousarray(f(inp["pre_g"])[0].reshape(8, 128).T),
        postg=np.ascontiguousarray(np.broadcast_to(f(inp["post_g"])[0][None, :], (128, DM))),
        convw=convw,
        hd=np.ascontiguousarray(f(inp["hyena_d"])[0].reshape(4, 128).T),
        sink=np.ascontiguousarray(np.broadcast_to(sink[None, :], (128, 8))),
        wf1=f(inp["w_f1"])[0], wf2=f(inp["w_f2"])[0], wf3=f(inp["w_f3"])[0], wf4=f(inp["w_f4"])[0],
        fb=np.ascontiguousarray(np.stack([f(inp["b_f1"])[0], f(inp["b_f2"])[0], f(inp["b_f3"])[0]], axis=1)),
        fr=np.ascontiguousarray(f(inp["sin_freq"])[0].T),
        zT=c["zT"], decay=c["decay"], fw=c["fw"], gw=c["gw"], kmats=c["kmats"], dtab=c["dtab"],
    )
    return d


def kernel(**inputs):
    x = np.asarray(inputs["x"], dtype=np.float32)
    shared = prep_shared(inputs)
    nc, _ = build()
    in_maps = []
    for c in range(NCORE):
        m = dict(shared)
        m["x"] = np.ascontiguousarray(x[c * NB:(c + 1) * NB])
        in_maps.append(m)
    res = run_bass_kernel_spmd(nc, in_maps, core_ids=list(range(NCORE)))
    return np.concatenate([r["out"] for r in res.results], axis=0).astype(np.float32)
```
